# Optimizing a Trainium2 kernel written in Bass

```python
import jax, jax.numpy as jnp
from jax import lax
import numpy as np

D_MODEL = 2048
BATCH = 8
SEQ = 4096
DEPTH = 2

N_MIXERS = 2
HG_HEADS = 16
HG_KEY_DIM = 128
HG_VAL_DIM = D_MODEL // HG_HEADS
HG_WIDTH = HG_HEADS * HG_KEY_DIM
HG_V_WIDTH = HG_HEADS * HG_VAL_DIM
HG_CHUNK = 64
SG_WIDTH = D_MODEL
SG_GROUPS = 16
SG_GROUP_DIM = SG_WIDTH // SG_GROUPS
SG_CHUNK = 128
D_FF = 5632
CONV_WIDTH = 3
ALPHA = (2 * DEPTH) ** 0.25
BETA = (8 * DEPTH) ** -0.25
LN_EPS = 1e-5
RMS_EPS = 1e-6
N_HG_LAYERS = (DEPTH + 1) // 2
N_SG_LAYERS = DEPTH // 2

kernel_name = 'hgrn2_gmlp_convffn_deepnorm_hybrid'


def layer_norm(x, g, b):
    xf = x.astype(jnp.float32)
    mu = jnp.mean(xf, axis=-1, keepdims=True)
    xc = xf - mu
    var = jnp.mean(xc * xc, axis=-1, keepdims=True)
    y = xc * lax.rsqrt(var + LN_EPS) * g.astype(jnp.float32) + b.astype(jnp.float32)
    return y.astype(x.dtype)


def hgrn2_mixer(x, w_in, norm_g, w_out, lb):
    b_, s_, _ = x.shape
    n = s_ // HG_CHUNK
    proj = x @ w_in
    q, f, i, g = jnp.split(proj, [HG_WIDTH, 2 * HG_WIDTH, 2 * HG_WIDTH + HG_V_WIDTH], axis=-1)
    f = f.astype(jnp.float32)
    q = jax.nn.silu(q.astype(jnp.float32))
    v = i.astype(jnp.float32)
    log_forget = jnp.logaddexp(jnp.log(lb), jnp.log1p(-lb) + jax.nn.log_sigmoid(f))
    k = (1.0 - lb) * jax.nn.sigmoid(-f)

    def to_chunks(t, d):
        return t.reshape(b_, n, HG_CHUNK, HG_HEADS, d).transpose(1, 0, 3, 2, 4)

    qc = to_chunks(q, HG_KEY_DIM)
    kc = to_chunks(k, HG_KEY_DIM)
    lfc = to_chunks(log_forget, HG_KEY_DIM)
    vc = to_chunks(v, HG_VAL_DIM)
    mask = jnp.tril(jnp.ones((HG_CHUNK, HG_CHUNK), dtype=bool))

    def step(state, inp):
        q_c, k_c, v_c, lf_c = inp
        cum = jnp.cumsum(lf_c, axis=2)
        rel = cum[:, :, :, None, :] - cum[:, :, None, :, :]
        decay = jnp.exp(jnp.where(mask[:, :, None], rel, -jnp.inf))
        scores = jnp.einsum('bhtk,bhsk,bhtsk->bhts', q_c, k_c, decay)
        out = (jnp.einsum('bhts,bhsv->bhtv', scores, v_c)
               + jnp.einsum('bhtk,bhkv->bhtv', q_c * jnp.exp(cum), state))
        last = cum[:, :, -1:, :]
        new_state = (jnp.exp(last[:, :, 0, :, None]) * state
                     + jnp.einsum('bhsk,bhsv->bhkv', k_c * jnp.exp(last - cum), v_c))
        return new_state, out

    state0 = jnp.zeros((b_, HG_HEADS, HG_KEY_DIM, HG_VAL_DIM), jnp.float32)
    _, o = lax.scan(step, state0, (qc, kc, vc, lfc))
    o = o.transpose(1, 0, 3, 2, 4).reshape(b_, s_, HG_HEADS, HG_VAL_DIM)
    o = o * lax.rsqrt(jnp.mean(o * o, axis=-1, keepdims=True) + RMS_EPS)
    o = o * norm_g.astype(jnp.float32).reshape(HG_HEADS, HG_VAL_DIM)
    o = o.reshape(b_, s_, HG_V_WIDTH) * jax.nn.silu(g.astype(jnp.float32))
    return o.astype(x.dtype) @ w_out


def chunked_gmlp(x, w_in, ln_g, ln_b, w_s, b_s, w_out):
    b_, s_, _ = x.shape
    n = s_ // SG_CHUNK
    z = jax.nn.gelu(x @ w_in, approximate=False)
    u, v = jnp.split(z, 2, axis=-1)
    v = layer_norm(v, ln_g, ln_b).reshape(b_, n, SG_CHUNK, SG_GROUPS, SG_GROUP_DIM)
    w_causal = w_s * jnp.tril(jnp.ones((SG_CHUNK, SG_CHUNK), w_s.dtype))
    gate = jnp.einsum('gts,bnsgc->bntgc', w_causal, v) + b_s.T[:, :, None]
    y = u * gate.reshape(b_, s_, SG_WIDTH)
    return y @ w_out


def conv_ffn(x, w_up, conv_w, conv_b, w_down):
    s_ = x.shape[1]
    h = x @ w_up
    a, b = jnp.split(h, 2, axis=-1)
    a_pad = jnp.pad(a, ((0, 0), (CONV_WIDTH - 1, 0), (0, 0)))
    a = sum(conv_w[j] * a_pad[:, j:j + s_] for j in range(CONV_WIDTH)) + conv_b
    return (jax.nn.silu(a) * b) @ w_down


def setup_inputs(seed: int = 0) -> dict:
    key = jax.random.key(seed)
    ks = jax.random.split(key, 24)
    nrm = jax.random.normal
    f32 = jnp.float32
    d = D_MODEL
    hg_in_cols = 2 * HG_WIDTH + 2 * HG_V_WIDTH
    return {
        'x': nrm(ks[0], (BATCH, SEQ, d), f32),
        'lb_logits': 0.5 * nrm(ks[1], (DEPTH + 1, HG_WIDTH), f32),
        'hg_w_in': nrm(ks[2], (N_HG_LAYERS, d, hg_in_cols), f32) * d ** -0.5,
        'hg_norm_g': 1.0 + 0.02 * nrm(ks[3], (N_HG_LAYERS, HG_V_WIDTH), f32),
        'hg_w_out': nrm(ks[4], (N_HG_LAYERS, HG_V_WIDTH, d), f32) * HG_V_WIDTH ** -0.5 * BETA,
        'sg_w_in': nrm(ks[5], (N_SG_LAYERS, d, 2 * SG_WIDTH), f32) * d ** -0.5,
        'sg_ln_g': 1.0 + 0.02 * nrm(ks[6], (N_SG_LAYERS, SG_WIDTH), f32),
        'sg_ln_b': 0.02 * nrm(ks[7], (N_SG_LAYERS, SG_WIDTH), f32),
        'sg_w_s': nrm(ks[8], (N_SG_LAYERS, SG_GROUPS, SG_CHUNK, SG_CHUNK), f32) * 0.5 * SG_CHUNK ** -0.5,
        'sg_b_s': 1.0 + 0.1 * nrm(ks[9], (N_SG_LAYERS, SG_GROUPS, SG_CHUNK), f32),
        'sg_w_out': nrm(ks[10], (N_SG_LAYERS, SG_WIDTH, d), f32) * SG_WIDTH ** -0.5 * BETA,
        'ffn_w_up': nrm(ks[11], (DEPTH, d, 2 * D_FF), f32) * d ** -0.5,
        'ffn_conv_w': nrm(ks[12], (DEPTH, CONV_WIDTH, D_FF), f32) * CONV_WIDTH ** -0.5,
        'ffn_conv_b': 0.02 * nrm(ks[13], (DEPTH, D_FF), f32),
        'ffn_w_down': nrm(ks[14], (DEPTH, D_FF, d), f32) * D_FF ** -0.5 * BETA,
        'ln1_g': 1.0 + 0.02 * nrm(ks[15], (DEPTH, d), f32),
        'ln1_b': 0.02 * nrm(ks[16], (DEPTH, d), f32),
        'ln2_g': 1.0 + 0.02 * nrm(ks[17], (DEPTH, d), f32),
        'ln2_b': 0.02 * nrm(ks[18], (DEPTH, d), f32),
    }


def reference(x, lb_logits, hg_w_in, hg_norm_g, hg_w_out, sg_w_in, sg_ln_g, sg_ln_b,
              sg_w_s, sg_b_s, sg_w_out, ffn_w_up, ffn_conv_w, ffn_conv_b, ffn_w_down,
              ln1_g, ln1_b, ln2_g, ln2_b):
    lower_bounds = jnp.cumsum(jax.nn.softmax(lb_logits.astype(jnp.float32), axis=0), axis=0)
    h = x
    for layer in range(DEPTH):
        occ = layer // N_MIXERS
        if layer % N_MIXERS == 0:
            mixed = hgrn2_mixer(h, hg_w_in[occ], hg_norm_g[occ], hg_w_out[occ], lower_bounds[layer])
        else:
            mixed = chunked_gmlp(h, sg_w_in[occ], sg_ln_g[occ], sg_ln_b[occ],
                                 sg_w_s[occ], sg_b_s[occ], sg_w_out[occ])
        h = layer_norm(ALPHA * h + mixed, ln1_g[layer], ln1_b[layer])
        ffn = conv_ffn(h, ffn_w_up[layer], ffn_conv_w[layer], ffn_conv_b[layer], ffn_w_down[layer])
        h = layer_norm(ALPHA * h + ffn, ln2_g[layer], ln2_b[layer])
    return h
```

```python
import numpy as np
import concourse.bass as bass
import concourse.mybir as mybir
from concourse.bass_utils import run_bass_kernel_spmd

F32 = mybir.dt.float32
BF16 = mybir.dt.bfloat16
AF = mybir.ActivationFunctionType
ALU = mybir.AluOpType

D = 2048
NDC = 16
S_FULL = 4096
T = 512
DFF = 5632
NFC = 44
WC = 256
NSLOT = 3
ALPHA = 4.0 ** 0.25
LN_EPS = 1e-5
RMS_EPS = 1e-6

V_LBL = 0
V_NG = V_LBL + 48
V_SGG = V_NG + 16
V_SGB = V_SGG + 16
V_LN1G = V_SGB + 16
V_LN1B = V_LN1G + 32
V_LN2G = V_LN1B + 32
V_LN2B = V_LN2G + 32
V_CW = V_LN2B + 32
V_CB = V_CW + 264
NV = V_CB + 88
C_ID = 0
C_ONES = 128
C_TRI = 256
NCST = 384
CB_SCAN = 0
CB_MASKT = 512
NCSTB = 1024


class Buf:
    __slots__ = ("name", "lw", "rd", "rd_dma")

    def __init__(self, name):
        self.name = name
        self.lw = None
        self.rd = {}
        self.rd_dma = []


class Op:
    __slots__ = ("eng", "fn", "deps", "signal", "semval", "is_dma", "dsem", "dval")


class Prog:
    ENGS = ("pe", "act", "dve", "pool", "sp")

    def __init__(self):
        self.q = {e: [] for e in self.ENGS}
        self.dma_counts = {}
        self.out_dmas = []

    def add(self, eng, fn, reads=(), writes=(), dma=None, is_out=False, nowaw=()):
        op = Op()
        op.eng = eng
        op.fn = fn
        op.signal = False
        op.semval = None
        op.is_dma = dma is not None
        if op.is_dma:
            self.dma_counts[dma] = self.dma_counts.get(dma, 0) + 16
            op.dsem = dma
            op.dval = self.dma_counts[dma]
        cand = []
        for b in reads:
            if b.lw is not None:
                cand.append((b.lw, "raw"))
        for b in writes:
            if b.lw is not None and b not in nowaw:
                cand.append((b.lw, "waw"))
            for r in b.rd.values():
                cand.append((r, "war"))
            for r in b.rd_dma:
                cand.append((r, "war"))
        deps = []
        for d, kind in cand:
            if d is op:
                continue
            if d.is_dma:
                deps.append(d)
            elif d.eng == eng and (eng == "pe" or kind != "raw"):
                continue
            else:
                d.signal = True
                deps.append(d)
        op.deps = deps
        for b in writes:
            b.lw = op
            b.rd = {}
            b.rd_dma = []
        for b in reads:
            if op.is_dma:
                b.rd_dma.append(op)
            else:
                b.rd[eng] = op
        self.q[eng].append(op)
        if is_out:
            self.out_dmas.append(op)
        return op

    def emit(self, nc, sems, dsems):
        for e in self.ENGS:
            n = 0
            for op in self.q[e]:
                if (not op.is_dma) and op.signal:
                    n += 1
                    op.semval = n
        handles = {"pe": nc.tensor, "act": nc.scalar, "dve": nc.vector,
                   "pool": nc.gpsimd, "sp": nc.sync}

        def run(e, h):
            waited = {}
            for op in self.q[e]:
                need = {}
                for d in op.deps:
                    if d.is_dma:
                        k = ("d", d.dsem)
                        v = d.dval
                    else:
                        k = ("e", d.eng)
                        v = d.semval
                    if v > need.get(k, 0):
                        need[k] = v
                for k, v in need.items():
                    if waited.get(k, 0) >= v:
                        continue
                    waited[k] = v
                    sem = dsems[k[1]] if k[0] == "d" else sems[k[1]]
                    h.wait_ge(sem, v)
                ins = op.fn(h)
                if op.is_dma:
                    ins.then_inc(dsems[op.dsem], 16)
                elif op.signal:
                    ins.then_inc(sems[op.eng], 1)
            if e == "sp":
                for op in self.out_dmas:
                    h.wait_ge(dsems[op.dsem], op.dval)

        with nc.Block() as block:
            @block.tensor
            def _(h):
                run("pe", h)

            @block.scalar
            def _(h):
                run("act", h)

            @block.vector
            def _(h):
                run("dve", h)

            @block.gpsimd
            def _(h):
                run("pool", h)

            @block.sync
            def _(h):
                run("sp", h)


def build(nt=8, dbg=None):
    S = nt * T
    nc = bass.Bass("TRN2", target_bir_lowering=False)
    P = Prog()

    def dram_in(name, shape):
        return nc.dram_tensor(name, list(shape), F32, kind="ExternalInput").ap()

    xT = dram_in("xT", (D, S))
    consts_d = dram_in("consts", (128, NCST))
    constsb_d = dram_in("constsb", (128, NCSTB))
    vecs_d = dram_in("vecs", (128, NV))
    wsT_d = dram_in("sg_w_sT", (128, 16 * 128))
    bs_d = dram_in("sg_b_row", (1, 2048))
    wspec = [("hg_in", D, 8192), ("hg_out", D, D), ("up0", D, 2 * DFF), ("dn0", DFF, D),
             ("sg_in", D, 4096), ("sg_out", D, D), ("up1", D, 2 * DFF), ("dn1", DFF, D)]
    w32 = {}
    w16 = {}
    wconv = {}
    for name, r, c in wspec:
        w32[name] = dram_in("w_" + name, (r, c))
        w16[name] = nc.dram_tensor("w16_" + name, [r, c], BF16, kind="Internal").ap()
    outT = nc.dram_tensor("outT", [D, S], F32, kind="ExternalOutput").ap()
    dbg_out = None

    def sb(name, shape, dt):
        return nc.alloc_sbuf_tensor("s_" + name, list(shape), dt)

    hT = sb("hT", (128, NDC, T), F32)
    hTb = sb("hTb", (128, NDC, T), BF16)
    X = sb("X", (128, 12288), F32)
    gated = X[:, 0:11264].bitcast(BF16).rearrange("p (c t) -> p c t", t=T)
    uT = X[:, 0:8192].rearrange("p (c t) -> p c t", t=T)
    yT = X[:, 8192:12288].bitcast(BF16).rearrange("p (c t) -> p c t", t=T)
    YW = 9216
    Y = sb("Y", (128, YW), F32)
    wslots = [sb(f"wslot{i}", (128, 16, WC), BF16) for i in range(NSLOT)]
    state = sb("state", (128, 16, 128), F32)
    state_bf = sb("state_bf", (128, 4, 128), BF16)
    cst = sb("cst", (128, NCST), F32)
    cstb = sb("cstb", (128, NCSTB), BF16)
    vecs = sb("vecs", (128, NV), F32)
    ident_bf = sb("ident_bf", (128, 128), BF16)
    ones_bf = sb("ones_bf", (128, 128), BF16)
    wcT = sb("wcT", (128, 16, 128), BF16)
    wsT_f = Y[:, 0:2048].rearrange("p (g t) -> p g t", t=128)
    bs_row = sb("bs_row", (1, 2048), F32)
    ones_f = sb("ones_f", (1, 128), F32)
    eps_col = sb("eps_col", (128, 1), F32)
    eps_ln = sb("eps_ln", (128, 1), F32)
    carry = sb("carry", (128, 4, NFC, 2), F32)
    lbv = sb("lbv", (128, 4, 16), F32)
    lbt = sb("lbt", (128, 8, 16), F32)
    ln_mean = sb("ln_mean", (128, T), F32)
    ln_var = sb("ln_var", (128, T), F32)
    ln_rstd = ln_var
    ln_B = sb("ln_B", (128, T), F32)
    ln_rb = [sb(f"ln_rb{i}", (128, T), BF16) for i in range(4)]
    ln_rsq = [sb(f"ln_rsq{i}", (128, T), BF16) for i in range(4)]
    ln_t = [sb(f"ln_t{i}", (128, T), F32) for i in range(2)]

    NB = 5
    pbank = [nc.alloc_psum_tensor(f"pb{i}", [128, 512], F32) for i in range(NB)]
    pbank_b = [Buf(f"pb{i}") for i in range(NB)]
    pstat = [nc.alloc_psum_tensor(f"pstat{i}", [128, 512], F32) for i in range(2)]
    pstat_b = [Buf(f"pstat{i}") for i in range(2)]
    pbt = nc.alloc_psum_tensor("pbt", [128, 1024], BF16)
    pbt_b = Buf("pbt")
    bank_rr = [0]

    def get_bank():
        i = bank_rr[0] % NB
        bank_rr[0] += 1
        return pbank[i], pbank_b[i]

    B_hT = [Buf(f"hT{c}") for c in range(NDC)]
    B_hTb = [Buf(f"hTb{c}") for c in range(NDC)]
    B_y = [Buf(f"y{c}") for c in range(NDC)]
    B_u = [Buf(f"u{c}") for c in range(NDC)]
    B_gated = [Buf(f"g{c}") for c in range(NFC)]
    B_slot = [Buf(f"slot{i}") for i in range(NSLOT)]
    B_state = [Buf(f"st{h}") for h in range(16)]
    B_statebf = [Buf(f"stb{h}") for h in range(4)]
    B_cst = Buf("cst")
    B_cstb = Buf("cstb")
    B_vecs = Buf("vecs")
    B_misc = Buf("misc")
    B_carry = [[Buf(f"carry{p}_{c}") for c in range(NFC)] for p in range(4)]
    B_w16 = {}

    def vcol(i):
        return vecs[:, i:i + 1]

    def load_xb(ti):
        for g in range(4):
            src = xT.rearrange("(c p) t -> p c t", p=128)[:, 4 * g:4 * g + 4, ti * T:(ti + 1) * T]
            P.add("pool", lambda h, g=g, src=src: h.dma_start(out=hTb[:, 4 * g:4 * g + 4, :], in_=src),
                  writes=B_hTb[4 * g:4 * g + 4], dma=f"xb{g}")

    def load_x(ti):
        for g in range(4):
            src = xT.rearrange("(c p) t -> p c t", p=128)[:, 4 * g:4 * g + 4, ti * T:(ti + 1) * T]
            P.add("sp", lambda h, g=g, src=src: h.dma_start(out=hT[:, 4 * g:4 * g + 4, :], in_=src),
                  writes=B_hT[4 * g:4 * g + 4], dma=f"x{g}")

    load_xb(0)
    load_x(0)
    P.add("sp", lambda h: h.dma_start(out=cst[:, :], in_=consts_d[:, :]), writes=[B_cst], dma="c0")
    P.add("sp", lambda h: h.dma_start(out=vecs[:, :], in_=vecs_d[:, :]), writes=[B_vecs], dma="c1")
    P.add("pool", lambda h: h.dma_start(out=cstb[:, :], in_=constsb_d[:, :]), writes=[B_cstb], dma="c4")
    B_ws = Buf("wsTf")
    P.add("sp", lambda h: h.dma_start(out=Y[:, 0:2048], in_=wsT_d[:, :]), writes=[B_ws], dma="c2")
    P.add("sp", lambda h: h.dma_start(out=bs_row[:, :], in_=bs_d[:, :]), writes=[B_misc], dma="c3")

    RPD = 256
    B_cvchain = Buf("cvchain")

    def conv_weight(name):
        r, c = w32[name].shape
        e = 2048 if c % 2048 == 0 else 1024
        bl = []
        for r0 in range(0, r, RPD):
            rn = min(RPD, r - r0)
            b = Buf(f"w16_{name}_{r0}")
            src = w32[name][r0:r0 + rn, :].rearrange("r (a e) -> r a e", e=e)
            dst = w16[name][r0:r0 + rn, :].rearrange("r (a e) -> r a e", e=e)
            P.add("pool", lambda h, s=src, d=dst: h.dma_start(out=d, in_=s), writes=[b, B_cvchain],
                  dma=f"cv_{name}")
            bl.append(b)
        B_w16[name] = bl

    names_l = [["hg_in", "hg_out", "up0", "dn0"], ["sg_in", "sg_out", "up1", "dn1"]]
    import os
    B_w16all = {n_: Buf("w16all_" + n_) for n_ in names_l[0] + names_l[1]}
    cur_tile = [0]
    B_wbslot = [Buf(f"wbslot{i}") for i in range(NSLOT)]

    B_id = Buf("identbf")
    P.add("dve", lambda h: h.memset(eps_col[:, :], float(128.0 * RMS_EPS)), writes=[B_id])
    P.add("dve", lambda h: h.memset(eps_ln[:, :], float(LN_EPS)), writes=[B_id])
    P.add("dve", lambda h: h.tensor_copy(out=ident_bf[:, :], in_=cst[:, C_ID:C_ID + 128]),
          reads=[B_cst], writes=[B_id])
    P.add("dve", lambda h: h.tensor_copy(out=ones_bf[:, :], in_=cst[:, C_ONES:C_ONES + 128]),
          reads=[B_cst], writes=[B_id])
    P.add("dve", lambda h: h.tensor_copy(out=ones_f[:, :], in_=cst[0:1, C_ONES:C_ONES + 128]),
          reads=[B_cst], writes=[B_id])
    P.add("dve", lambda h: h.memset(state[:, :, :], 0.0), writes=B_state)
    P.add("dve", lambda h: h.memset(carry[:, :, :, :], 0.0), writes=B_carry[0] + B_carry[1] + B_carry[2] + B_carry[3])
    B_lb = Buf("lb")
    l0 = vecs[:, V_LBL:V_LBL + 16]
    l1 = vecs[:, V_LBL + 16:V_LBL + 32]
    l2 = vecs[:, V_LBL + 32:V_LBL + 48]
    P.add("dve", lambda h: h.tensor_max(out=lbt[:, 0, :], in0=l0, in1=l1), reads=[B_vecs], writes=[B_lb])
    P.add("dve", lambda h: h.tensor_max(out=lbt[:, 1, :], in0=lbt[:, 0, :], in1=l2), reads=[B_lb, B_vecs], writes=[B_lb])
    for r_, l_ in enumerate((l0, l1, l2)):
        P.add("dve", lambda h, r_=r_, l_=l_: h.tensor_sub(out=lbt[:, 2 + r_, :], in0=l_, in1=lbt[:, 1, :]),
              reads=[B_lb, B_vecs], writes=[B_lb])
    P.add("act", lambda h: h.activation(out=lbt[:, 5:8, :], in_=lbt[:, 2:5, :], func=AF.Exp),
          reads=[B_lb], writes=[B_lb])
    P.add("dve", lambda h: h.tensor_add(out=lbt[:, 0, :], in0=lbt[:, 5, :], in1=lbt[:, 6, :]), reads=[B_lb], writes=[B_lb])
    P.add("dve", lambda h: h.tensor_add(out=lbt[:, 1, :], in0=lbt[:, 0, :], in1=lbt[:, 7, :]), reads=[B_lb], writes=[B_lb])
    P.add("dve", lambda h: h.reciprocal(out=lbt[:, 0, :], in_=lbt[:, 1, :]), reads=[B_lb], writes=[B_lb])
    P.add("dve", lambda h: h.tensor_tensor(out=lbv[:, 0, :], in0=lbt[:, 5, :], in1=lbt[:, 0, :], op=ALU.mult),
          reads=[B_lb], writes=[B_lb])
    P.add("dve", lambda h: h.tensor_scalar(out=lbv[:, 1, :], in0=lbv[:, 0, :], scalar1=-1.0, scalar2=1.0,
                                           op0=ALU.mult, op1=ALU.add), reads=[B_lb], writes=[B_lb])
    P.add("dve", lambda h: h.tensor_scalar(out=lbv[:, 2, :], in0=vecs[:, V_NG:V_NG + 16],
                                           scalar1=float(np.sqrt(128.0)), scalar2=None, op0=ALU.mult),
          reads=[B_vecs, B_lb], writes=[B_lb])
    B_wc = Buf("wcT")
    for g in range(16):
        P.add("dve", lambda h, g=g: h.tensor_tensor(out=wcT[:, g, :], in0=wsT_f[:, g, :],
                                                    in1=cst[:, C_TRI:C_TRI + 128], op=ALU.mult),
              reads=[B_ws, B_cst], writes=[B_wc])
    B_Y = Buf("Yscratch")
    P.add("dve", lambda h: h.memset(Y[:, 8448:9216], 0.0), reads=[B_wc], writes=[B_Y, B_ws])

    slot_rr = [0]

    def load_piece(name, k0, nk, c0):
        i = slot_rr[0] % NSLOT
        slot_rr[0] += 1
        dview = w16[name][k0 * 128:(k0 + nk) * 128, c0:c0 + WC].rearrange("(k p) c -> p k c", p=128)
        if cur_tile[0] == 0:
            src = w32[name][k0 * 128:(k0 + nk) * 128, c0:c0 + WC].rearrange("(k p) c -> p k c", p=128)
            P.add("pool", lambda h, i=i, src=src, nk=nk: h.dma_start(out=wslots[i][:, 0:nk, :], in_=src),
                  writes=[B_slot[i]], dma=f"slot{i}")
            P.add("sp", lambda h, i=i, dview=dview, nk=nk: h.dma_start(out=dview, in_=wslots[i][:, 0:nk, :]),
                  reads=[B_slot[i]], writes=[B_wbslot[i]], nowaw=[B_wbslot[i]], dma=f"wbslot{i}")
        else:
            P.add("sp", lambda h, i=i, dview=dview, nk=nk: h.dma_start(out=wslots[i][:, 0:nk, :], in_=dview),
                  reads=B_wbslot, writes=[B_slot[i]], dma=f"slot{i}")
        return wslots[i], B_slot[i]

    def mm_group(bank, bank_b, slot, slot_b, m, k0, nk, ktot, act_ap, act_bufs):
        for kk in range(nk):
            k = k0 + kk
            P.add("pe", lambda h, kk=kk, k=k: h.matmul(bank[:, :], lhsT=slot[:, kk, m * 128:(m + 1) * 128],
                                                      rhs=act_ap[:, k, :], start=(k == 0), stop=(k == ktot - 1)),
                  reads=[slot_b, act_bufs[k]], writes=[bank_b])

    def mm_kouter(groups, act_ap, act_bufs):
        for k in range(16):
            for (bank, bank_b, slot, slot_b, m) in groups:
                P.add("pe", lambda h, k=k, bank=bank, slot=slot, m=m: h.matmul(
                    bank[:, :], lhsT=slot[:, k, m * 128:(m + 1) * 128], rhs=act_ap[:, k, :],
                    start=(k == 0), stop=(k == 15)), reads=[slot_b, act_bufs[k]], writes=[bank_b])

    def ln_stats_begin():
        return (pstat[0], pstat_b[0], pstat[1], pstat_b[1])

    ln_rr = [0]

    def ln_stats_add(st, src_ap, src_buf, c, n):
        s1, s1b, s2, s2b = st
        i = st_rr[0] % 4
        st_rr[0] += 1
        rb, rsq = ln_rb[i], ln_rsq[i]
        brb, brsq = B_lnrb[i], B_lnrsq[i]
        P.add("act", lambda h: h.activation(out=rb[:, :], in_=src_ap, func=AF.Copy), reads=[src_buf], writes=[brb])
        P.add("act", lambda h: h.activation(out=rsq[:, :], in_=src_ap, func=AF.Square), reads=[src_buf], writes=[brsq])

        def pe_part():
            P.add("pe", lambda h: h.matmul(s1[:, :], lhsT=ones_bf[:, :], rhs=rb[:, :], start=(c == 0), stop=(c == n - 1)),
                  reads=[brb, B_id], writes=[s1b])
            P.add("pe", lambda h: h.matmul(s2[:, :], lhsT=ones_bf[:, :], rhs=rsq[:, :], start=(c == 0), stop=(c == n - 1)),
                  reads=[brsq, B_id], writes=[s2b])
        return pe_part

    st_rr = [0]
    B_lnrb = [Buf(f"lnrb{i}") for i in range(4)]
    B_lnrsq = [Buf(f"lnrsq{i}") for i in range(4)]
    B_ln = Buf("lnstat")
    B_lnt = [Buf("lnt0"), Buf("lnt1")]

    def ln_stats_finish(st, nfeat, eps):
        s1, s1b, s2, s2b = st
        inv = 1.0 / nfeat
        P.add("dve", lambda h: h.tensor_scalar(out=ln_mean[:, :], in0=s1[:, :], scalar1=inv, scalar2=None, op0=ALU.mult),
              reads=[s1b], writes=[B_ln])
        P.add("dve", lambda h: h.tensor_tensor(out=ln_var[:, :], in0=ln_mean[:, :], in1=ln_mean[:, :], op=ALU.mult),
              reads=[B_ln], writes=[B_ln])
        P.add("dve", lambda h: h.scalar_tensor_tensor(out=ln_var[:, :], in0=s2[:, :], scalar=inv, in1=ln_var[:, :],
                                                      op0=ALU.mult, op1=ALU.subtract), reads=[s2b, B_ln], writes=[B_ln])
        P.add("act", lambda h: h.activation(out=ln_var[:, :], in_=ln_var[:, :], func=AF.Ln, bias=eps_ln[:, 0:1], scale=1.0),
              reads=[B_ln, B_id], writes=[B_ln])
        P.add("act", lambda h: h.activation(out=ln_rstd[:, :], in_=ln_var[:, :], func=AF.Exp, scale=-0.5),
              reads=[B_ln], writes=[B_ln])
        P.add("dve", lambda h: h.scalar_tensor_tensor(out=ln_B[:, :], in0=ln_mean[:, :], scalar=-1.0, in1=ln_rstd[:, :],
                                                      op0=ALU.mult, op1=ALU.mult), reads=[B_ln], writes=[B_ln])

    def ln_apply(src_ap, src_buf, gcol, bcol, outs):
        i = ln_rr[0] % 2
        ln_rr[0] += 1
        t = ln_t[i]
        bt = B_lnt[i]
        P.add("dve", lambda h: h.tensor_tensor(out=t[:, :], in0=src_ap, in1=ln_rstd[:, :], op=ALU.mult),
              reads=[src_buf, B_ln], writes=[bt])
        P.add("dve", lambda h: h.tensor_tensor(out=t[:, :], in0=t[:, :], in1=ln_B[:, :], op=ALU.add),
              reads=[bt, B_ln], writes=[bt])
        for ap, buf in outs:
            P.add("act", lambda h, ap=ap: h.activation(out=ap, in_=t[:, :], func=AF.Identity, bias=bcol, scale=gcol),
                  reads=[bt, B_vecs], writes=[buf])

    def out_proj_ln(wname, nk, act_ap, act_bufs, vg, vb, write_hTb=True):
        st = ln_stats_begin()
        pending = []
        kparts = [(k0, min(16, nk - k0)) for k0 in range(0, nk, 16)]
        for cg in range(D // WC):
            banks = [get_bank() for _ in range(WC // 128)]
            for (k0, kn) in kparts:
                slot, slot_b = load_piece(wname, k0, kn, cg * WC)
                for m in range(WC // 128):
                    mm_group(banks[m][0], banks[m][1], slot, slot_b, m, k0, kn, nk, act_ap, act_bufs)
            for f_ in pending:
                f_()
            newp = []
            for m in range(WC // 128):
                dc = cg * (WC // 128) + m
                bk, bkb = banks[m]
                P.add("dve", lambda h, dc=dc, bk=bk: h.scalar_tensor_tensor(
                    out=hT[:, dc, :], in0=hT[:, dc, :], scalar=float(ALPHA), in1=bk[:, :],
                    op0=ALU.mult, op1=ALU.add), reads=[B_hT[dc], bkb], writes=[B_hT[dc]])
                newp.append(ln_stats_add(st, hT[:, dc, :], B_hT[dc], dc, NDC))
            pending = newp
        for f_ in pending:
            f_()
        ln_stats_finish(st, D, LN_EPS)
        for dc in range(NDC):
            outs = [(hTb[:, dc, :], B_hTb[dc])] if write_hTb else []
            outs.append((hT[:, dc, :], B_hT[dc]))
            ln_apply(hT[:, dc, :], B_hT[dc], vcol(vg + dc), vcol(vb + dc), outs)

    FW = T + 2
    f_abuf = [Y[:, i * 2048:i * 2048 + FW] for i in range(2)]
    f_t1 = [Y[:, 1024 + i * 2048:1024 + i * 2048 + T] for i in range(2)]
    f_s = [Y[:, 4096 + i * 512:4096 + (i + 1) * 512] for i in range(2)]
    B_fa = [Buf("fa0"), Buf("fa1")]
    B_ft = [Buf("ft0"), Buf("ft1")]
    B_fs = [Buf("fs0"), Buf("fs1")]

    def ffn(layer, ti, mid_hook=None, write_hTb=True):
        up = f"up{layer}"
        par = 2 * layer + ti % 2
        par2 = 2 * layer + (ti + 1) % 2

        pre = {}
        grp = []
        for c in (0, 1):
            slot, slot_b = load_piece(up, 0, 16, c * WC)
            a_ps, a_b = get_bank()
            b_ps, b_b = get_bank()
            pre[c] = (a_ps, a_b, b_ps, b_b)
            grp += [(a_ps, a_b, slot, slot_b, 0), (b_ps, b_b, slot, slot_b, 1)]
        mm_kouter(grp, hTb, B_hTb)

        def ffn_c(c):
            i = c % 2
            if c in pre:
                a_ps, a_b, b_ps, b_b = pre[c]
            else:
                slot, slot_b = load_piece(up, 0, 16, c * WC)
                a_ps, a_b = get_bank()
                b_ps, b_b = get_bank()
                mm_group(a_ps, a_b, slot, slot_b, 0, 0, 16, 16, hTb, B_hTb)
                mm_group(b_ps, b_b, slot, slot_b, 1, 0, 16, 16, hTb, B_hTb)
            ab = f_abuf[i]
            t1 = f_t1[i]
            s_ = f_s[i]
            cw = V_CW + layer * 132
            P.add("act", lambda h, ab=ab, a_ps=a_ps: h.activation(out=ab[:, 2:FW], in_=a_ps[:, :], func=AF.Copy),
                  reads=[a_b], writes=[B_fa[i]])
            P.add("act", lambda h, ab=ab, c=c: h.activation(out=ab[:, 0:2], in_=carry[:, par, c, :], func=AF.Copy),
                  reads=[B_carry[par][c]], writes=[B_fa[i]])
            P.add("act", lambda h, a_ps=a_ps, c=c: h.activation(out=carry[:, par2, c, :], in_=a_ps[:, T - 2:T], func=AF.Copy),
                  reads=[a_b], writes=[B_carry[par2][c]])
            P.add("dve", lambda h, ab=ab, t1=t1, c=c: h.tensor_scalar(
                out=t1, in0=ab[:, 2:FW], scalar1=vcol(cw + 2 * NFC + c), scalar2=None, op0=ALU.mult),
                reads=[B_fa[i], B_vecs], writes=[B_ft[i]])
            P.add("dve", lambda h, ab=ab, t1=t1, c=c: h.scalar_tensor_tensor(
                out=t1, in0=ab[:, 1:FW - 1], scalar=vcol(cw + NFC + c), in1=t1, op0=ALU.mult, op1=ALU.add),
                reads=[B_fa[i], B_ft[i], B_vecs], writes=[B_ft[i]])
            P.add("dve", lambda h, ab=ab, t1=t1, c=c: h.scalar_tensor_tensor(
                out=t1, in0=ab[:, 0:T], scalar=vcol(cw + c), in1=t1, op0=ALU.mult, op1=ALU.add),
                reads=[B_fa[i], B_ft[i], B_vecs], writes=[B_ft[i]])
            P.add("act", lambda h, t1=t1, s_=s_, c=c: h.activation(
                out=s_, in_=t1, func=AF.Silu, bias=vcol(V_CB + layer * NFC + c), scale=1.0),
                reads=[B_ft[i], B_vecs], writes=[B_fs[i]])
            P.add("dve", lambda h, s_=s_, b_ps=b_ps, c=c: h.tensor_tensor(
                out=gated[:, c, :], in0=s_, in1=b_ps[:, :], op=ALU.mult),
                reads=[B_fs[i], b_b], writes=[B_gated[c]])

        for c in range(NFC):
            ffn_c(c)
        if mid_hook is not None:
            mid_hook()
        out_proj_ln(f"dn{layer}", NFC, gated, B_gated, V_LN2G + layer * 16, V_LN2B + layer * 16, write_hTb=write_hTb)

    def ybuf(off, n=T):
        return Y[:, off:off + n]

    hg_fg = [ybuf(0), ybuf(512)]
    hg_cum = [ybuf(1024), ybuf(1536)]
    hg_kT = [ybuf(2048), ybuf(2560)]
    hg_e1 = [ybuf(3072), ybuf(3584)]
    hg_e2 = [ybuf(4096), ybuf(4608)]
    hg_qs = [ybuf(5120), ybuf(5632)]
    hg_gs = [ybuf(6144), ybuf(6656)]
    def ybf(off, n=T):
        return Y[:, off:off + n // 2].bitcast(BF16)

    hg_qt = [ybf(7168), ybf(7424)]
    hg_kt = [ybf(7680), ybf(7936)]
    hg_ib = [ybf(8192), ybf(8192)]
    hg_sq_t = sb("hg_sq", (128, T), BF16)
    hg_sq = [hg_sq_t[:, :], hg_sq_t[:, :]]
    hg_tok = [Y[:, 8448:9216].bitcast(BF16), Y[:, 8448:9216].bitcast(BF16)]
    hg_sm = ln_rb[3]
    hg_el = sb("hg_el", (128, 2, 8), F32)
    hg_tmp = ln_rsq[3][:, :].bitcast(F32).rearrange("p (a b) -> p a b", b=128)
    hg_rs = ln_t[0]
    hg_A = ln_t[1]
    names = ["fg", "cum", "kT", "e1", "e2", "qs", "gs", "qt", "kt"]
    B_hg = {n: [Buf(f"hg_{n}0"), Buf(f"hg_{n}1")] for n in names}
    for n in ["ib", "tok"]:
        b_ = Buf(f"hg_{n}")
        B_hg[n] = [b_, b_]
    b_ = Buf("hg_sq")
    B_hg["sq"] = [b_, b_]
    B_sm = Buf("hg_sm")
    B_el = [Buf("hg_el0"), Buf("hg_el1")]
    B_tmp = [Buf("hg_tmp0"), Buf("hg_tmp1")]
    B_rs = Buf("hg_rs")
    B_A = Buf("hg_A")

    import os
    HGS = int(os.environ.get('HGS', '99'))
    NOOUT = int(os.environ.get('NOOUT', '0'))

    def hgrn2(mid_hook=None):
        PB = [(pbank[j], pbank_b[j]) for j in range(4)]
        D0, D1, OB = (pbank[4], pbank_b[4]), (pstat[0], pstat_b[0]), (pstat[1], pstat_b[1])
        B_dsb = [Buf("dsb0"), Buf("dsb1")]
        dsb = [ln_mean, ln_var]
        tk = hg_tok[0]
        P.add("dve", lambda h: h.memset(tk[64:128, 512:1024], 0.0), reads=[B_Y], writes=[B_hg["tok"][0]])
        P.add("dve", lambda h: h.memset(tk[0:64, 1024:1536], 0.0), reads=[B_Y], writes=[B_hg["tok"][0]])

        def proj_slices(hd):
            sA, sAb = load_piece("hg_in", 0, 16, hd * 512)
            sB, sBb = load_piece("hg_in", 0, 16, hd * 512 + 256)
            order = [(PB[1], sA, sAb, 1), (PB[0], sA, sAb, 0), (PB[2], sB, sBb, 0), (PB[3], sB, sBb, 1)]
            sl = []
            for (bank, bank_b), slot, slot_b, m in order:
                for half in range(4):
                    def f_(bank=bank, bank_b=bank_b, slot=slot, slot_b=slot_b, m=m, half=half):
                        for k in range(half * 4, half * 4 + 4):
                            P.add("pe", lambda h, k=k: h.matmul(bank[:, :], lhsT=slot[:, k, m * 128:(m + 1) * 128],
                                                               rhs=hTb[:, k, :], start=(k == 0), stop=(k == 15)),
                                  reads=[slot_b, B_hTb[k]], writes=[bank_b])
                    sl.append(f_)
            return sl

        def gate_stages(hd):
            i = hd % 2
            (q_ps, q_b), (f_ps, f_b), (i_ps, i_b), (g_ps, g_b) = PB
            fg, cum, kT, e1, e2, qs, gs = hg_fg[i], hg_cum[i], hg_kT[i], hg_e1[i], hg_e2[i], hg_qs[i], hg_gs[i]
            qt, kt, ib = hg_qt[i], hg_kt[i], hg_ib[i]
            Bf = {n: B_hg[n][i] for n in B_hg}
            lbc = lbv[:, 0, hd:hd + 1]
            lb1 = lbv[:, 1, hd:hd + 1]
            el = hg_el[:, i, :]

            def ev():
                P.add("act", lambda h: h.activation(out=fg, in_=f_ps[:, :], func=AF.Sigmoid),
                      reads=[f_b, B_Y], writes=[Bf["fg"]])
                P.add("act", lambda h: h.activation(out=qs, in_=q_ps[:, :], func=AF.Silu), reads=[q_b], writes=[Bf["qs"]])
                P.add("act", lambda h: h.activation(out=ib, in_=i_ps[:, :], func=AF.Copy), reads=[i_b], writes=[Bf["ib"]])
                P.add("act", lambda h: h.activation(out=gs, in_=g_ps[:, :], func=AF.Silu), reads=[g_b], writes=[Bf["gs"]])

            def restA():
                P.add("dve", lambda h: h.tensor_scalar(out=fg, in0=fg, scalar1=lb1, scalar2=lbc, op0=ALU.mult, op1=ALU.add),
                      reads=[Bf["fg"], B_lb], writes=[Bf["fg"]])
                P.add("act", lambda h: h.activation(out=cum, in_=fg, func=AF.Ln), reads=[Bf["fg"]], writes=[Bf["cum"]])
                P.add("dve", lambda h: h.tensor_scalar(out=kT, in0=fg, scalar1=-1.0, scalar2=1.0, op0=ALU.mult, op1=ALU.add),
                      reads=[Bf["fg"]], writes=[Bf["kT"]])
                P.add("dve", lambda h: h.tensor_tensor_scan(out=cum, data0=cstb[:, CB_SCAN:CB_SCAN + T], data1=cum,
                                                            initial=0.0, op0=ALU.mult, op1=ALU.add),
                      reads=[Bf["cum"], B_cstb], writes=[Bf["cum"]])

            def restB():
                P.add("act", lambda h: h.activation(out=e1, in_=cum, func=AF.Exp), reads=[Bf["cum"]], writes=[Bf["e1"]])
                P.add("act", lambda h: h.activation(out=e2, in_=cum, func=AF.Exp, scale=-1.0),
                      reads=[Bf["cum"]], writes=[Bf["e2"]])
                P.add("act", lambda h: h.activation(out=el, in_=e1.rearrange("p (c s) -> p c s", s=64)[:, :, 63], func=AF.Copy),
                      reads=[Bf["e1"]], writes=[B_el[i]])

            def restC():
                P.add("dve", lambda h: h.tensor_tensor(out=kt, in0=kT, in1=e2, op=ALU.mult),
                      reads=[Bf["kT"], Bf["e2"]], writes=[Bf["kt"]])
                P.add("dve", lambda h: h.tensor_tensor(out=qt, in0=qs, in1=e1, op=ALU.mult),
                      reads=[Bf["qs"], Bf["e1"]], writes=[Bf["qt"]])
            return ev, restA, restB, restC

        def tphase(hd):
            i = hd % 2
            qt, kt, ib, tok = hg_qt[i], hg_kt[i], hg_ib[i], hg_tok[i]
            Bf = {n: B_hg[n][i] for n in B_hg}
            for tb in range(4):
                P.add("pe", lambda h, tb=tb: h.transpose(out=pbt[:, tb * 128:(tb + 1) * 128],
                                                         in_=ib[:, tb * 128:(tb + 1) * 128], identity=ident_bf[:, :]),
                      reads=[Bf["ib"], B_id], writes=[pbt_b])
            for tb in range(4):
                P.add("pe", lambda h, tb=tb: h.transpose(out=pbt[:, 512 + tb * 128:512 + (tb + 1) * 128],
                                                         in_=kt[:, tb * 128:(tb + 1) * 128], identity=ident_bf[:, :]),
                      reads=[Bf["kt"], B_id], writes=[pbt_b])
            P.add("dve", lambda h: h.tensor_copy(out=tok[:, 0:512], in_=pbt[:, 0:512]), reads=[pbt_b], writes=[Bf["tok"]])
            P.add("dve", lambda h: h.tensor_copy(out=tok[0:64, 512:1024], in_=pbt[0:64, 512:1024]),
                  reads=[pbt_b], writes=[Bf["tok"]])
            P.add("dve", lambda h: h.tensor_copy(out=tok[64:128, 1024:1536], in_=pbt[64:128, 512:1024]),
                  reads=[pbt_b], writes=[Bf["tok"]])
            P.add("act", lambda h: h.activation(out=state_bf[:, 0, :], in_=state[:, hd, :], func=AF.Copy),
                  reads=[B_state[hd]], writes=[B_statebf[0]])
            s_ps, s_b = OB
            for j in range(4):
                P.add("pe", lambda h, j=j: h.matmul(s_ps[:, j * 128:(j + 1) * 128], lhsT=kt[:, j * 128:(j + 1) * 128],
                                                    rhs=qt[:, j * 128:(j + 1) * 128], start=True, stop=True),
                      reads=[Bf["kt"], Bf["qt"]], writes=[s_b])
            P.add("dve", lambda h: h.tensor_tensor(out=hg_sm[:, :], in0=s_ps[:, :], in1=cstb[:, CB_MASKT:CB_MASKT + 512],
                                                   op=ALU.mult), reads=[s_b, B_cstb], writes=[B_sm])
            for c in range(8):
                p, j = c % 2, c // 2
                dp, dpb = (D0, D1)[c // 4]
                P.add("pe", lambda h, c=c, p=p, j=j, dp=dp: h.matmul(
                    dp[:, (c % 4) * 128:(c % 4 + 1) * 128],
                    lhsT=tok[:, 512 + p * 512 + j * 128:512 + p * 512 + (j + 1) * 128],
                    rhs=tok[:, j * 128:(j + 1) * 128], start=True, stop=True), reads=[Bf["tok"]], writes=[dpb])

        def cphase(hd, nslices, nstages):
            i = hd % 2
            qt, tok = hg_qt[i], hg_tok[i]
            Bf = {n: B_hg[n][i] for n in B_hg}
            o_ps, o_b = OB
            for c in range(8):
                p, j = c % 2, c // 2
                dp, dpb = (D0, D1)[c // 4]
                if p == 0:
                    P.add("pe", lambda h, j=j: h.matmul(o_ps[:, j * 128:(j + 1) * 128], lhsT=tok[:, j * 128:(j + 1) * 128],
                                                        rhs=hg_sm[:, j * 128:(j + 1) * 128], start=True, stop=False),
                          reads=[Bf["tok"], B_sm], writes=[o_b])
                P.add("pe", lambda h, c=c: h.matmul(o_ps[:, c * 64:(c + 1) * 64], lhsT=state_bf[:, c % 4, :],
                                                    rhs=qt[:, c * 64:(c + 1) * 64], start=False, stop=True),
                      reads=[B_statebf[c % 4], Bf["qt"]], writes=[o_b])
                if nslices is not None:
                    nslices[c]()
                tm = hg_tmp[:, c % 2, :]
                P.add("dve", lambda h, c=c, dp=dp, tm=tm: h.tensor_tensor(
                    out=tm, in0=dp[:, (c % 4) * 128:(c % 4 + 1) * 128], in1=state[:, hd, :], op=ALU.add),
                    reads=[dpb, B_state[hd]], writes=[B_tmp[c % 2]])
                esc = hg_el[:, i, c:c + 1]
                if c < 7:
                    P.add("act", lambda h, c=c, tm=tm, esc=esc: h.activation(
                        out=state_bf[:, (c + 1) % 4, :], in_=tm, func=AF.Copy, scale=esc),
                        reads=[B_tmp[c % 2], B_el[i]], writes=[B_statebf[(c + 1) % 4]])
                P.add("dve", lambda h, tm=tm, esc=esc: h.tensor_scalar(
                    out=state[:, hd, :], in0=tm, scalar1=esc, scalar2=None, op0=ALU.mult),
                    reads=[B_tmp[c % 2], B_el[i]], writes=[B_state[hd]])

        def r1(hd):
            i = hd % 2
            sq = hg_sq[i]
            Bf = {n: B_hg[n][i] for n in B_hg}
            o_ps, o_b = OB
            ss_ps, ss_b = D0
            P.add("act", lambda h: h.activation(out=sq, in_=o_ps[:, :], func=AF.Square), reads=[o_b], writes=[Bf["sq"]])
            P.add("pe", lambda h: h.matmul(ss_ps[:, :], lhsT=ones_bf[:, :], rhs=sq, start=True, stop=True),
                  reads=[Bf["sq"], B_id], writes=[ss_b])

        def r2a(hd):
            ss_ps, ss_b = D0
            P.add("act", lambda h: h.activation(out=hg_rs[:, :], in_=ss_ps[:, :], func=AF.Ln,
                                                bias=eps_col[:, 0:1], scale=1.0), reads=[ss_b, B_id], writes=[B_rs])
            P.add("act", lambda h: h.activation(out=hg_rs[:, :], in_=hg_rs[:, :], func=AF.Exp, scale=-0.5),
                  reads=[B_rs], writes=[B_rs])

        def r2b(hd):
            i = hd % 2
            gs = hg_gs[i]
            Bf = {n: B_hg[n][i] for n in B_hg}
            o_ps, o_b = OB
            P.add("dve", lambda h: h.tensor_tensor(out=hg_A[:, :], in0=o_ps[:, :], in1=hg_rs[:, :], op=ALU.mult),
                  reads=[o_b, B_rs], writes=[B_A])
            ngc = lbv[:, 2, hd:hd + 1]
            P.add("dve", lambda h: h.scalar_tensor_tensor(out=yT[:, hd, :], in0=hg_A[:, :], scalar=ngc, in1=gs,
                                                          op0=ALU.mult, op1=ALU.mult),
                  reads=[B_A, Bf["gs"], B_lb], writes=[B_y[hd]])

        for f_ in proj_slices(0):
            f_()
        gates = {hd: gate_stages(hd) for hd in range(16)}
        for f_ in gates[0]:
            f_()
        for f_ in proj_slices(1):
            f_()
        for hd in range(16):
            tphase(hd)
            nx = gates[hd + 1] if hd + 1 < 16 else None
            if nx is not None:
                nx[0]()
            nsl = proj_slices(hd + 2) if hd + 2 < 16 else None
            cphase(hd, nsl[0:8] if nsl is not None else None, None)
            r1(hd)
            if nsl is not None:
                for f_ in nsl[8:]:
                    f_()
            if nx is not None:
                nx[1]()
            r2a(hd)
            if nx is not None:
                nx[2]()
            r2b(hd)
            if nx is not None:
                nx[3]()
            if hd == 3 and mid_hook is not None:
                mid_hook()
        out_proj_ln("hg_out", 16, yT, B_y, V_LN1G, V_LN1B)

    vT = Y[:, 0:8192].rearrange("p (c t) -> p c t", t=T)
    B_v = [Buf(f"v{c}") for c in range(16)]
    g_vn = [Y[:, 8192:8448].bitcast(BF16), Y[:, 8448:8704].bitcast(BF16)]
    g_tok = [Y[:, 8704:8960].bitcast(BF16), Y[:, 8960:9216].bitcast(BF16)]
    B_gvn = [Buf("gvn0"), Buf("gvn1")]
    B_gtok = [Buf("gtok0"), Buf("gtok1")]
    g_t2 = [ln_t[0], ln_t[1]]

    def gmlp():
        st = ln_stats_begin()
        pre = {}
        grp = []
        for j in (0, 1):
            slot, slot_b = load_piece("sg_in", 0, 16, j * WC)
            b0 = get_bank()
            b1 = get_bank()
            pre[j] = (b0, b1)
            grp += [(b0[0], b0[1], slot, slot_b, 0), (b1[0], b1[1], slot, slot_b, 1)]
        mm_kouter(grp, hTb, B_hTb)
        pend = []
        for j in range(8):
            if j in pre:
                banks = pre[j]
            else:
                slot, slot_b = load_piece("sg_in", 0, 16, j * WC)
                banks = (get_bank(), get_bank())
                for m in range(2):
                    mm_group(banks[m][0], banks[m][1], slot, slot_b, m, 0, 16, 16, hTb, B_hTb)
            newp = []
            for m in range(2):
                c = 2 * j + m
                v_ps, v_b = banks[m]
                P.add("act", lambda h, c=c, v_ps=v_ps: h.activation(out=vT[:, c, :], in_=v_ps[:, :], func=AF.Gelu),
                      reads=[v_b, B_Y], writes=[B_v[c]])
                newp.append(ln_stats_add(st, vT[:, c, :], B_v[c], c, 16))
            for f_ in pend:
                f_()
            pend = newp
        for f_ in pend:
            f_()
        ln_stats_finish(st, D, LN_EPS)

        def lnap(j):
            for m in range(2):
                c = 2 * j + m
                ln_apply(vT[:, c, :], B_v[c], vcol(V_SGG + c), vcol(V_SGB + c), [(g_vn[m], B_gvn[m])])

        tok2 = Y[:, 8704:9216].bitcast(BF16)
        lnap(0)

        def pair(j):
            for m in range(2):
                vn = g_vn[m]
                for tb in range(4):
                    P.add("pe", lambda h, tb=tb, vn=vn, m=m: h.transpose(
                        out=pbt[:, m * 512 + tb * 128:m * 512 + (tb + 1) * 128],
                        in_=vn[:, tb * 128:(tb + 1) * 128], identity=ident_bf[:, :]),
                        reads=[B_gvn[m], B_id], writes=[pbt_b])
            P.add("dve", lambda h: h.tensor_copy(out=tok2, in_=pbt[:, :]), reads=[pbt_b], writes=[B_gtok[0], B_gtok[1]])
            if j + 1 < 8:
                lnap(j + 1)
            slot, slot_b = load_piece("sg_in", 0, 16, 2048 + j * WC)
            ub = (get_bank(), get_bank())
            for m in range(2):
                mm_group(ub[m][0], ub[m][1], slot, slot_b, m, 0, 16, 16, hTb, B_hTb)
            for m in range(2):
                c = 2 * j + m
                u_ps, u_b = ub[m]
                P.add("act", lambda h, c=c, u_ps=u_ps: h.activation(out=uT[:, c, :], in_=u_ps[:, :], func=AF.Gelu),
                      reads=[u_b], writes=[B_u[c]])
            gb = (get_bank(), get_bank())
            for m in range(2):
                c = 2 * j + m
                g_ps, g_b = gb[m]
                for tb in range(4):
                    P.add("pe", lambda h, tb=tb, m=m, c=c, g_ps=g_ps: h.matmul(
                        g_ps[:, tb * 128:(tb + 1) * 128], lhsT=tok2[:, m * 512 + tb * 128:m * 512 + (tb + 1) * 128],
                        rhs=wcT[:, c, :], start=True, stop=False), reads=[B_gtok[m], B_wc], writes=[g_b])
                    P.add("pe", lambda h, tb=tb, c=c, g_ps=g_ps: h.matmul(
                        g_ps[:, tb * 128:(tb + 1) * 128], lhsT=ones_f[:, :], rhs=bs_row[:, c * 128:(c + 1) * 128],
                        start=False, stop=True), reads=[B_misc, B_id], writes=[g_b])
            for m in range(2):
                c = 2 * j + m
                g_ps, g_b = gb[m]
                P.add("dve", lambda h, c=c, g_ps=g_ps: h.tensor_tensor(out=yT[:, c, :], in0=g_ps[:, :], in1=uT[:, c, :],
                                                                      op=ALU.mult), reads=[g_b, B_u[c]], writes=[B_y[c]])

        for j in range(8):
            pair(j)
        out_proj_ln("sg_out", 16, yT, B_y, V_LN1G + 16, V_LN1B + 16)

    def cast_h():
        for dc in range(NDC):
            P.add("act", lambda h, dc=dc: h.activation(out=hTb[:, dc, :], in_=hT[:, dc, :], func=AF.Copy),
                  reads=[B_hT[dc]], writes=[B_hTb[dc]])

    nlayers = 4 if dbg is None else dbg[0]
    assert nlayers == 4
    for ti in range(nt):
        cur_tile[0] = ti
        hgrn2(mid_hook=(lambda ti=ti: load_x(ti)) if ti > 0 else None)
        ffn(0, ti)
        gmlp()
        hook = (lambda ti=ti: load_xb(ti + 1)) if ti + 1 < nt else None
        ffn(1, ti, mid_hook=hook, write_hTb=False)
        for g in range(4):
            dst = outT.rearrange("(c p) t -> p c t", p=128)[:, 4 * g:4 * g + 4, ti * T:(ti + 1) * T]
            P.add("act", lambda h, g=g, dst=dst: h.dma_start(out=dst, in_=hT[:, 4 * g:4 * g + 4, :]),
                  reads=B_hT[4 * g:4 * g + 4], dma=f"o{g}", is_out=True)

    import contextlib
    with contextlib.ExitStack() as es:
        sems = {e: es.enter_context(nc.semaphore("sem_" + e)) for e in Prog.ENGS}
        dsems = {k: es.enter_context(nc.semaphore("dsem_" + str(n))) for n, k in enumerate(P.dma_counts)}
        P.emit(nc, sems, dsems)
    return nc


def _cols(v, nch):
    return np.ascontiguousarray(np.asarray(v, np.float32).reshape(nch, 128).T)


def make_consts():
    c = np.zeros((128, NCST), np.float32)
    c[:, C_ID:C_ID + 128] = np.eye(128, dtype=np.float32)
    c[:, C_ONES:C_ONES + 128] = 1.0
    s = np.arange(128)[:, None]
    tt = np.arange(128)[None, :]
    c[:, C_TRI:C_TRI + 128] = (s <= tt).astype(np.float32)
    return c


def make_constsb():
    c = np.zeros((128, NCSTB), np.float32)
    p = np.arange(128)[:, None]
    t = np.arange(128)[None, :]
    m = ((p <= t) & ((p // 64) == (t // 64))).astype(np.float32)
    c[:, CB_MASKT:CB_MASKT + 512] = np.tile(m, (1, 4))
    sc = np.ones((128, 512), np.float32)
    sc[:, ::64] = 0.0
    c[:, CB_SCAN:CB_SCAN + 512] = sc
    return c


def prep_shared(inp):
    f = lambda a: np.asarray(a, np.float32)
    d = {}
    d["consts"] = make_consts()
    d["constsb"] = make_constsb()
    parts = []
    lbl = f(inp["lb_logits"])
    parts += [_cols(lbl[r], 16) for r in range(3)]
    parts.append(_cols(f(inp["hg_norm_g"])[0], 16))
    parts.append(_cols(f(inp["sg_ln_g"])[0], 16))
    parts.append(_cols(f(inp["sg_ln_b"])[0], 16))
    for k in ("ln1_g", "ln1_b", "ln2_g", "ln2_b"):
        a = f(inp[k])
        parts += [_cols(a[0], 16), _cols(a[1], 16)]
    cw = f(inp["ffn_conv_w"])
    for l in range(2):
        for j in range(3):
            parts.append(_cols(cw[l, j], NFC))
    cb = f(inp["ffn_conv_b"])
    parts += [_cols(cb[0], NFC), _cols(cb[1], NFC)]
    vec = np.concatenate(parts, axis=1)
    assert vec.shape == (128, NV), vec.shape
    d["vecs"] = np.ascontiguousarray(vec)
    ws = f(inp["sg_w_s"])[0]
    d["sg_w_sT"] = np.ascontiguousarray(ws.transpose(2, 0, 1).reshape(128, 16 * 128))
    d["sg_b_row"] = np.ascontiguousarray(f(inp["sg_b_s"])[0].reshape(1, 2048))
    w = f(inp["hg_w_in"])[0]
    d["w_hg_in"] = np.ascontiguousarray(w.reshape(D, 4, 16, 128).transpose(0, 2, 1, 3).reshape(D, 8192))
    d["w_hg_out"] = np.ascontiguousarray(f(inp["hg_w_out"])[0])
    w = f(inp["sg_w_in"])[0]
    d["w_sg_in"] = np.ascontiguousarray(np.concatenate([w[:, 2048:], w[:, :2048]], axis=1))
    d["w_sg_out"] = np.ascontiguousarray(f(inp["sg_w_out"])[0])
    for l in range(2):
        w = f(inp["ffn_w_up"])[l]
        d[f"w_up{l}"] = np.ascontiguousarray(w.reshape(D, 2, NFC, 128).transpose(0, 2, 1, 3).reshape(D, 2 * DFF))
        d[f"w_dn{l}"] = np.ascontiguousarray(f(inp["ffn_w_down"])[l])
    return d


_NC_CACHE = {}


def kernel(**inputs):
    x = np.asarray(inputs["x"], np.float32)
    B = x.shape[0]
    shared = prep_shared(inputs)
    if "nc" not in _NC_CACHE:
        _NC_CACHE["nc"] = build(S_FULL // T)
    nc = _NC_CACHE["nc"]
    in_maps = []
    for b in range(B):
        m = dict(shared)
        m["xT"] = np.ascontiguousarray(x[b].T)
        in_maps.append(m)
    res = run_bass_kernel_spmd(nc, in_maps, core_ids=list(range(B)))
    out = np.stack([np.ascontiguousarray(res.results[b]["outT"].T) for b in range(B)], axis=0)
    return out.astype(np.float32)
```

```python
import numpy as np
import concourse.bass as bass
import concourse.mybir as mybir
from concourse.bass_utils import run_bass_kernel_spmd

F32 = mybir.dt.float32
BF16 = mybir.dt.bfloat16
AF = mybir.ActivationFunctionType
ALU = mybir.AluOpType

D = 2048
NDC = 16
S_FULL = 4096
T = 512
DFF = 5632
NFC = 44
WC = 256
NSLOT = 3
ALPHA = 4.0 ** 0.25
LN_EPS = 1e-5
RMS_EPS = 1e-6

V_LBL = 0
V_NG = V_LBL + 48
V_SGG = V_NG + 16
V_SGB = V_SGG + 16
V_LN1G = V_SGB + 16
V_LN1B = V_LN1G + 32
V_LN2G = V_LN1B + 32
V_LN2B = V_LN2G + 32
V_CW = V_LN2B + 32
V_CB = V_CW + 264
NV = V_CB + 88
C_ID = 0
C_ONES = 128
C_TRI = 256
NCST = 384
CB_SCAN = 0
CB_MASKT = 512
NCSTB = 1024


class Buf:
    __slots__ = ("name", "lw", "rd", "rd_dma")

    def __init__(self, name):
        self.name = name
        self.lw = None
        self.rd = {}
        self.rd_dma = []


class Op:
    __slots__ = ("eng", "fn", "deps", "signal", "semval", "is_dma", "dsem", "dval")


class Prog:
    ENGS = ("pe", "act", "dve", "pool", "sp")

    def __init__(self):
        self.q = {e: [] for e in self.ENGS}
        self.dma_counts = {}
        self.out_dmas = []

    def add(self, eng, fn, reads=(), writes=(), dma=None, is_out=False, nowaw=()):
        op = Op()
        op.eng = eng
        op.fn = fn
        op.signal = False
        op.semval = None
        op.is_dma = dma is not None
        if op.is_dma:
            self.dma_counts[dma] = self.dma_counts.get(dma, 0) + 16
            op.dsem = dma
            op.dval = self.dma_counts[dma]
        cand = []
        for b in reads:
            if b.lw is not None:
                cand.append((b.lw, "raw"))
        for b in writes:
            if b.lw is not None and b not in nowaw:
                cand.append((b.lw, "waw"))
            for r in b.rd.values():
                cand.append((r, "war"))
            for r in b.rd_dma:
                cand.append((r, "war"))
        deps = []
        for d, kind in cand:
            if d is op:
                continue
            if d.is_dma:
                deps.append(d)
            elif d.eng == eng and (eng == "pe" or kind != "raw"):
                continue
            else:
                d.signal = True
                deps.append(d)
        op.deps = deps
        for b in writes:
            b.lw = op
            b.rd = {}
            b.rd_dma = []
        for b in reads:
            if op.is_dma:
                b.rd_dma.append(op)
            else:
                b.rd[eng] = op
        self.q[eng].append(op)
        if is_out:
            self.out_dmas.append(op)
        return op

    def emit(self, nc, sems, dsems):
        for e in self.ENGS:
            n = 0
            for op in self.q[e]:
                if (not op.is_dma) and op.signal:
                    n += 1
                    op.semval = n
        handles = {"pe": nc.tensor, "act": nc.scalar, "dve": nc.vector,
                   "pool": nc.gpsimd, "sp": nc.sync}

        def run(e, h):
            waited = {}
            for op in self.q[e]:
                need = {}
                for d in op.deps:
                    if d.is_dma:
                        k = ("d", d.dsem)
                        v = d.dval
                    else:
                        k = ("e", d.eng)
                        v = d.semval
                    if v > need.get(k, 0):
                        need[k] = v
                for k, v in need.items():
                    if waited.get(k, 0) >= v:
                        continue
                    waited[k] = v
                    sem = dsems[k[1]] if k[0] == "d" else sems[k[1]]
                    h.wait_ge(sem, v)
                ins = op.fn(h)
                if op.is_dma:
                    ins.then_inc(dsems[op.dsem], 16)
                elif op.signal:
                    ins.then_inc(sems[op.eng], 1)
            if e == "sp":
                for op in self.out_dmas:
                    h.wait_ge(dsems[op.dsem], op.dval)

        with nc.Block() as block:
            @block.tensor
            def _(h):
                run("pe", h)

            @block.scalar
            def _(h):
                run("act", h)

            @block.vector
            def _(h):
                run("dve", h)

            @block.gpsimd
            def _(h):
                run("pool", h)

            @block.sync
            def _(h):
                run("sp", h)


def build(nt=8, dbg=None):
    S = nt * T
    nc = bass.Bass("TRN2", target_bir_lowering=False)
    P = Prog()

    def dram_in(name, shape):
        return nc.dram_tensor(name, list(shape), F32, kind="ExternalInput").ap()

    xT = dram_in("xT", (D, S))
    consts_d = dram_in("consts", (128, NCST))
    constsb_d = dram_in("constsb", (128, NCSTB))
    vecs_d = dram_in("vecs", (128, NV))
    wsT_d = dram_in("sg_w_sT", (128, 16 * 128))
    bs_d = dram_in("sg_b_row", (1, 2048))
    wspec = [("hg_in", D, 8192), ("hg_out", D, D), ("up0", D, 2 * DFF), ("dn0", DFF, D),
             ("sg_in", D, 4096), ("sg_out", D, D), ("up1", D, 2 * DFF), ("dn1", DFF, D)]
    w32 = {}
    w16 = {}
    wconv = {}
    for name, r, c in wspec:
        w32[name] = dram_in("w_" + name, (r, c))
        w16[name] = nc.dram_tensor("w16_" + name, [r, c], BF16, kind="Internal").ap()
    outT = nc.dram_tensor("outT", [D, S], F32, kind="ExternalOutput").ap()
    dbg_out = None

    def sb(name, shape, dt):
        return nc.alloc_sbuf_tensor("s_" + name, list(shape), dt)

    hT = sb("hT", (128, NDC, T), F32)
    hTb = sb("hTb", (128, NDC, T), BF16)
    X = sb("X", (128, 12288), F32)
    gated = X[:, 0:11264].bitcast(BF16).rearrange("p (c t) -> p c t", t=T)
    uT = X[:, 0:8192].rearrange("p (c t) -> p c t", t=T)
    yT = X[:, 8192:12288].bitcast(BF16).rearrange("p (c t) -> p c t", t=T)
    YW = 9216
    Y = sb("Y", (128, YW), F32)
    wslots = [sb(f"wslot{i}", (128, 16, WC), BF16) for i in range(NSLOT)]
    state = sb("state", (128, 16, 128), F32)
    state_bf = sb("state_bf", (128, 4, 128), BF16)
    cst = sb("cst", (128, NCST), F32)
    cstb = sb("cstb", (128, NCSTB), BF16)
    vecs = sb("vecs", (128, NV), F32)
    ident_bf = sb("ident_bf", (128, 128), BF16)
    ones_bf = sb("ones_bf", (128, 128), BF16)
    wcT = sb("wcT", (128, 16, 128), BF16)
    wsT_f = Y[:, 0:2048].rearrange("p (g t) -> p g t", t=128)
    bs_row = sb("bs_row", (1, 2048), F32)
    ones_f = sb("ones_f", (1, 128), F32)
    eps_col = sb("eps_col", (128, 1), F32)
    eps_ln = sb("eps_ln", (128, 1), F32)
    carry = sb("carry", (128, 4, NFC, 2), F32)
    lbv = sb("lbv", (128, 4, 16), F32)
    lbt = sb("lbt", (128, 8, 16), F32)
    ln_mean = sb("ln_mean", (128, T), F32)
    ln_var = sb("ln_var", (128, T), F32)
    ln_rstd = ln_var
    ln_B = sb("ln_B", (128, T), F32)
    ln_rb = [sb(f"ln_rb{i}", (128, T), BF16) for i in range(4)]
    ln_rsq = [sb(f"ln_rsq{i}", (128, T), BF16) for i in range(4)]
    ln_t = [sb(f"ln_t{i}", (128, T), F32) for i in range(2)]

    NB = 5
    pbank = [nc.alloc_psum_tensor(f"pb{i}", [128, 512], F32) for i in range(NB)]
    pbank_b = [Buf(f"pb{i}") for i in range(NB)]
    pstat = [nc.alloc_psum_tensor(f"pstat{i}", [128, 512], F32) for i in range(2)]
    pstat_b = [Buf(f"pstat{i}") for i in range(2)]
    pbt = nc.alloc_psum_tensor("pbt", [128, 1024], BF16)
    pbt_b = Buf("pbt")
    bank_rr = [0]

    def get_bank():
        i = bank_rr[0] % NB
        bank_rr[0] += 1
        return pbank[i], pbank_b[i]

    B_hT = [Buf(f"hT{c}") for c in range(NDC)]
    B_hTb = [Buf(f"hTb{c}") for c in range(NDC)]
    B_y = [Buf(f"y{c}") for c in range(NDC)]
    B_u = [Buf(f"u{c}") for c in range(NDC)]
    B_gated = [Buf(f"g{c}") for c in range(NFC)]
    B_slot = [Buf(f"slot{i}") for i in range(NSLOT)]
    B_state = [Buf(f"st{h}") for h in range(16)]
    B_statebf = [Buf(f"stb{h}") for h in range(4)]
    B_cst = Buf("cst")
    B_cstb = Buf("cstb")
    B_vecs = Buf("vecs")
    B_misc = Buf("misc")
    B_carry = [[Buf(f"carry{p}_{c}") for c in range(NFC)] for p in range(4)]
    B_w16 = {}

    def vcol(i):
        return vecs[:, i:i + 1]

    def load_xb(ti):
        for g in range(4):
            src = xT.rearrange("(c p) t -> p c t", p=128)[:, 4 * g:4 * g + 4, ti * T:(ti + 1) * T]
            P.add("pool", lambda h, g=g, src=src: h.dma_start(out=hTb[:, 4 * g:4 * g + 4, :], in_=src),
                  writes=B_hTb[4 * g:4 * g + 4], dma=f"xb{g}")

    def load_x(ti):
        for g in range(4):
            src = xT.rearrange("(c p) t -> p c t", p=128)[:, 4 * g:4 * g + 4, ti * T:(ti + 1) * T]
            P.add("sp", lambda h, g=g, src=src: h.dma_start(out=hT[:, 4 * g:4 * g + 4, :], in_=src),
                  writes=B_hT[4 * g:4 * g + 4], dma=f"x{g}")

    load_xb(0)
    load_x(0)
    P.add("sp", lambda h: h.dma_start(out=cst[:, :], in_=consts_d[:, :]), writes=[B_cst], dma="c0")
    P.add("sp", lambda h: h.dma_start(out=vecs[:, :], in_=vecs_d[:, :]), writes=[B_vecs], dma="c1")
    P.add("pool", lambda h: h.dma_start(out=cstb[:, :], in_=constsb_d[:, :]), writes=[B_cstb], dma="c4")
    B_ws = Buf("wsTf")
    P.add("sp", lambda h: h.dma_start(out=Y[:, 0:2048], in_=wsT_d[:, :]), writes=[B_ws], dma="c2")
    P.add("sp", lambda h: h.dma_start(out=bs_row[:, :], in_=bs_d[:, :]), writes=[B_misc], dma="c3")

    RPD = 256
    B_cvchain = Buf("cvchain")

    def conv_weight(name):
        r, c = w32[name].shape
        e = 2048 if c % 2048 == 0 else 1024
        bl = []
        for r0 in range(0, r, RPD):
            rn = min(RPD, r - r0)
            b = Buf(f"w16_{name}_{r0}")
            src = w32[name][r0:r0 + rn, :].rearrange("r (a e) -> r a e", e=e)
            dst = w16[name][r0:r0 + rn, :].rearrange("r (a e) -> r a e", e=e)
            P.add("pool", lambda h, s=src, d=dst: h.dma_start(out=d, in_=s), writes=[b, B_cvchain],
                  dma=f"cv_{name}")
            bl.append(b)
        B_w16[name] = bl

    names_l = [["hg_in", "hg_out", "up0", "dn0"], ["sg_in", "sg_out", "up1", "dn1"]]
    import os
    B_w16all = {n_: Buf("w16all_" + n_) for n_ in names_l[0] + names_l[1]}
    cur_tile = [0]
    B_wbslot = [Buf(f"wbslot{i}") for i in range(NSLOT)]

    B_id = Buf("identbf")
    P.add("dve", lambda h: h.memset(eps_col[:, :], float(128.0 * RMS_EPS)), writes=[B_id])
    P.add("dve", lambda h: h.memset(eps_ln[:, :], float(LN_EPS)), writes=[B_id])
    P.add("dve", lambda h: h.tensor_copy(out=ident_bf[:, :], in_=cst[:, C_ID:C_ID + 128]),
          reads=[B_cst], writes=[B_id])
    P.add("dve", lambda h: h.tensor_copy(out=ones_bf[:, :], in_=cst[:, C_ONES:C_ONES + 128]),
          reads=[B_cst], writes=[B_id])
    P.add("dve", lambda h: h.tensor_copy(out=ones_f[:, :], in_=cst[0:1, C_ONES:C_ONES + 128]),
          reads=[B_cst], writes=[B_id])
    P.add("dve", lambda h: h.memset(state[:, :, :], 0.0), writes=B_state)
    P.add("dve", lambda h: h.memset(carry[:, :, :, :], 0.0), writes=B_carry[0] + B_carry[1] + B_carry[2] + B_carry[3])
    B_lb = Buf("lb")
    l0 = vecs[:, V_LBL:V_LBL + 16]
    l1 = vecs[:, V_LBL + 16:V_LBL + 32]
    l2 = vecs[:, V_LBL + 32:V_LBL + 48]
    P.add("dve", lambda h: h.tensor_max(out=lbt[:, 0, :], in0=l0, in1=l1), reads=[B_vecs], writes=[B_lb])
    P.add("dve", lambda h: h.tensor_max(out=lbt[:, 1, :], in0=lbt[:, 0, :], in1=l2), reads=[B_lb, B_vecs], writes=[B_lb])
    for r_, l_ in enumerate((l0, l1, l2)):
        P.add("dve", lambda h, r_=r_, l_=l_: h.tensor_sub(out=lbt[:, 2 + r_, :], in0=l_, in1=lbt[:, 1, :]),
              reads=[B_lb, B_vecs], writes=[B_lb])
    P.add("act", lambda h: h.activation(out=lbt[:, 5:8, :], in_=lbt[:, 2:5, :], func=AF.Exp),
          reads=[B_lb], writes=[B_lb])
    P.add("dve", lambda h: h.tensor_add(out=lbt[:, 0, :], in0=lbt[:, 5, :], in1=lbt[:, 6, :]), reads=[B_lb], writes=[B_lb])
    P.add("dve", lambda h: h.tensor_add(out=lbt[:, 1, :], in0=lbt[:, 0, :], in1=lbt[:, 7, :]), reads=[B_lb], writes=[B_lb])
    P.add("dve", lambda h: h.reciprocal(out=lbt[:, 0, :], in_=lbt[:, 1, :]), reads=[B_lb], writes=[B_lb])
    P.add("dve", lambda h: h.tensor_tensor(out=lbv[:, 0, :], in0=lbt[:, 5, :], in1=lbt[:, 0, :], op=ALU.mult),
          reads=[B_lb], writes=[B_lb])
    P.add("dve", lambda h: h.tensor_scalar(out=lbv[:, 1, :], in0=lbv[:, 0, :], scalar1=-1.0, scalar2=1.0,
                                           op0=ALU.mult, op1=ALU.add), reads=[B_lb], writes=[B_lb])
    P.add("dve", lambda h: h.tensor_scalar(out=lbv[:, 2, :], in0=vecs[:, V_NG:V_NG + 16],
                                           scalar1=float(np.sqrt(128.0)), scalar2=None, op0=ALU.mult),
          reads=[B_vecs, B_lb], writes=[B_lb])
    B_wc = Buf("wcT")
    for g in range(16):
        P.add("dve", lambda h, g=g: h.tensor_tensor(out=wcT[:, g, :], in0=wsT_f[:, g, :],
                                                    in1=cst[:, C_TRI:C_TRI + 128], op=ALU.mult),
              reads=[B_ws, B_cst], writes=[B_wc])
    B_Y = Buf("Yscratch")
    P.add("dve", lambda h: h.memset(Y[:, 8448:9216], 0.0), reads=[B_wc], writes=[B_Y, B_ws])

    slot_rr = [0]

    def load_piece(name, k0, nk, c0):
        i = slot_rr[0] % NSLOT
        slot_rr[0] += 1
        dview = w16[name][k0 * 128:(k0 + nk) * 128, c0:c0 + WC].rearrange("(k p) c -> p k c", p=128)
        if cur_tile[0] == 0:
            src = w32[name][k0 * 128:(k0 + nk) * 128, c0:c0 + WC].rearrange("(k p) c -> p k c", p=128)
            P.add("pool", lambda h, i=i, src=src, nk=nk: h.dma_start(out=wslots[i][:, 0:nk, :], in_=src),
                  writes=[B_slot[i]], dma=f"slotsw{i}")
            P.add("sp", lambda h, i=i, dview=dview, nk=nk: h.dma_start(out=dview, in_=wslots[i][:, 0:nk, :]),
                  reads=[B_slot[i]], writes=[B_wbslot[i]], nowaw=[B_wbslot[i]], dma=f"wbslot{i}")
        else:
            P.add("sp", lambda h, i=i, dview=dview, nk=nk: h.dma_start(out=wslots[i][:, 0:nk, :], in_=dview),
                  reads=B_wbslot, writes=[B_slot[i]], dma=f"slot{i}")
        return wslots[i], B_slot[i]

    def mm_group(bank, bank_b, slot, slot_b, m, k0, nk, ktot, act_ap, act_bufs):
        for kk in range(nk):
            k = k0 + kk
            P.add("pe", lambda h, kk=kk, k=k: h.matmul(bank[:, :], lhsT=slot[:, kk, m * 128:(m + 1) * 128],
                                                      rhs=act_ap[:, k, :], start=(k == 0), stop=(k == ktot - 1)),
                  reads=[slot_b, act_bufs[k]], writes=[bank_b])

    def mm_kouter(groups, act_ap, act_bufs):
        for k in range(16):
            for (bank, bank_b, slot, slot_b, m) in groups:
                P.add("pe", lambda h, k=k, bank=bank, slot=slot, m=m: h.matmul(
                    bank[:, :], lhsT=slot[:, k, m * 128:(m + 1) * 128], rhs=act_ap[:, k, :],
                    start=(k == 0), stop=(k == 15)), reads=[slot_b, act_bufs[k]], writes=[bank_b])

    def ln_stats_begin():
        return (pstat[0], pstat_b[0], pstat[1], pstat_b[1])

    ln_rr = [0]

    def ln_stats_add(st, src_ap, src_buf, c, n):
        s1, s1b, s2, s2b = st
        i = st_rr[0] % 4
        st_rr[0] += 1
        rb, rsq = ln_rb[i], ln_rsq[i]
        brb, brsq = B_lnrb[i], B_lnrsq[i]
        P.add("act", lambda h: h.activation(out=rb[:, :], in_=src_ap, func=AF.Copy), reads=[src_buf], writes=[brb])
        P.add("act", lambda h: h.activation(out=rsq[:, :], in_=src_ap, func=AF.Square), reads=[src_buf], writes=[brsq])

        def pe_part():
            P.add("pe", lambda h: h.matmul(s1[:, :], lhsT=ones_bf[:, :], rhs=rb[:, :], start=(c == 0), stop=(c == n - 1)),
                  reads=[brb, B_id], writes=[s1b])
            P.add("pe", lambda h: h.matmul(s2[:, :], lhsT=ones_bf[:, :], rhs=rsq[:, :], start=(c == 0), stop=(c == n - 1)),
                  reads=[brsq, B_id], writes=[s2b])
        return pe_part

    st_rr = [0]
    B_lnrb = [Buf(f"lnrb{i}") for i in range(4)]
    B_lnrsq = [Buf(f"lnrsq{i}") for i in range(4)]
    B_ln = Buf("lnstat")
    B_lnt = [Buf("lnt0"), Buf("lnt1")]

    def ln_stats_finish(st, nfeat, eps):
        s1, s1b, s2, s2b = st
        inv = 1.0 / nfeat
        P.add("dve", lambda h: h.tensor_scalar(out=ln_mean[:, :], in0=s1[:, :], scalar1=inv, scalar2=None, op0=ALU.mult),
              reads=[s1b], writes=[B_ln])
        P.add("dve", lambda h: h.tensor_tensor(out=ln_var[:, :], in0=ln_mean[:, :], in1=ln_mean[:, :], op=ALU.mult),
              reads=[B_ln], writes=[B_ln])
        P.add("dve", lambda h: h.scalar_tensor_tensor(out=ln_var[:, :], in0=s2[:, :], scalar=inv, in1=ln_var[:, :],
                                                      op0=ALU.mult, op1=ALU.subtract), reads=[s2b, B_ln], writes=[B_ln])
        P.add("act", lambda h: h.activation(out=ln_var[:, :], in_=ln_var[:, :], func=AF.Ln, bias=eps_ln[:, 0:1], scale=1.0),
              reads=[B_ln, B_id], writes=[B_ln])
        P.add("act", lambda h: h.activation(out=ln_rstd[:, :], in_=ln_var[:, :], func=AF.Exp, scale=-0.5),
              reads=[B_ln], writes=[B_ln])
        P.add("dve", lambda h: h.scalar_tensor_tensor(out=ln_B[:, :], in0=ln_mean[:, :], scalar=-1.0, in1=ln_rstd[:, :],
                                                      op0=ALU.mult, op1=ALU.mult), reads=[B_ln], writes=[B_ln])

    def ln_apply(src_ap, src_buf, gcol, bcol, outs):
        i = ln_rr[0] % 2
        ln_rr[0] += 1
        t = ln_t[i]
        bt = B_lnt[i]
        P.add("dve", lambda h: h.tensor_tensor(out=t[:, :], in0=src_ap, in1=ln_rstd[:, :], op=ALU.mult),
              reads=[src_buf, B_ln], writes=[bt])
        P.add("dve", lambda h: h.tensor_tensor(out=t[:, :], in0=t[:, :], in1=ln_B[:, :], op=ALU.add),
              reads=[bt, B_ln], writes=[bt])
        for ap, buf in outs:
            P.add("act", lambda h, ap=ap: h.activation(out=ap, in_=t[:, :], func=AF.Identity, bias=bcol, scale=gcol),
                  reads=[bt, B_vecs], writes=[buf])

    def out_proj_ln(wname, nk, act_ap, act_bufs, vg, vb, write_hTb=True):
        st = ln_stats_begin()
        pending = []
        kparts = [(k0, min(16, nk - k0)) for k0 in range(0, nk, 16)]
        for cg in range(D // WC):
            banks = [get_bank() for _ in range(WC // 128)]
            for (k0, kn) in kparts:
                slot, slot_b = load_piece(wname, k0, kn, cg * WC)
                for m in range(WC // 128):
                    mm_group(banks[m][0], banks[m][1], slot, slot_b, m, k0, kn, nk, act_ap, act_bufs)
            for f_ in pending:
                f_()
            newp = []
            for m in range(WC // 128):
                dc = cg * (WC // 128) + m
                bk, bkb = banks[m]
                P.add("dve", lambda h, dc=dc, bk=bk: h.scalar_tensor_tensor(
                    out=hT[:, dc, :], in0=hT[:, dc, :], scalar=float(ALPHA), in1=bk[:, :],
                    op0=ALU.mult, op1=ALU.add), reads=[B_hT[dc], bkb], writes=[B_hT[dc]])
                newp.append(ln_stats_add(st, hT[:, dc, :], B_hT[dc], dc, NDC))
            pending = newp
        for f_ in pending:
            f_()
        ln_stats_finish(st, D, LN_EPS)
        for dc in range(NDC):
            outs = [(hTb[:, dc, :], B_hTb[dc])] if write_hTb else []
            outs.append((hT[:, dc, :], B_hT[dc]))
            ln_apply(hT[:, dc, :], B_hT[dc], vcol(vg + dc), vcol(vb + dc), outs)

    FW = T + 2
    f_abuf = [Y[:, i * 2048:i * 2048 + FW] for i in range(2)]
    f_t1 = [Y[:, 1024 + i * 2048:1024 + i * 2048 + T] for i in range(2)]
    f_s = [Y[:, 4096 + i * 512:4096 + (i + 1) * 512] for i in range(2)]
    B_fa = [Buf("fa0"), Buf("fa1")]
    B_ft = [Buf("ft0"), Buf("ft1")]
    B_fs = [Buf("fs0"), Buf("fs1")]

    def ffn(layer, ti, mid_hook=None, write_hTb=True):
        up = f"up{layer}"
        par = 2 * layer + ti % 2
        par2 = 2 * layer + (ti + 1) % 2

        pre = {}
        grp = []
        for c in (0, 1):
            slot, slot_b = load_piece(up, 0, 16, c * WC)
            a_ps, a_b = get_bank()
            b_ps, b_b = get_bank()
            pre[c] = (a_ps, a_b, b_ps, b_b)
            grp += [(a_ps, a_b, slot, slot_b, 0), (b_ps, b_b, slot, slot_b, 1)]
        mm_kouter(grp, hTb, B_hTb)

        def ffn_c(c):
            i = c % 2
            if c in pre:
                a_ps, a_b, b_ps, b_b = pre[c]
            else:
                slot, slot_b = load_piece(up, 0, 16, c * WC)
                a_ps, a_b = get_bank()
                b_ps, b_b = get_bank()
                mm_group(a_ps, a_b, slot, slot_b, 0, 0, 16, 16, hTb, B_hTb)
                mm_group(b_ps, b_b, slot, slot_b, 1, 0, 16, 16, hTb, B_hTb)
            ab = f_abuf[i]
            t1 = f_t1[i]
            s_ = f_s[i]
            cw = V_CW + layer * 132
            P.add("act", lambda h, ab=ab, a_ps=a_ps: h.activation(out=ab[:, 2:FW], in_=a_ps[:, :], func=AF.Copy),
                  reads=[a_b], writes=[B_fa[i]])
            P.add("act", lambda h, ab=ab, c=c: h.activation(out=ab[:, 0:2], in_=carry[:, par, c, :], func=AF.Copy),
                  reads=[B_carry[par][c]], writes=[B_fa[i]])
            P.add("act", lambda h, a_ps=a_ps, c=c: h.activation(out=carry[:, par2, c, :], in_=a_ps[:, T - 2:T], func=AF.Copy),
                  reads=[a_b], writes=[B_carry[par2][c]])
            P.add("dve", lambda h, ab=ab, t1=t1, c=c: h.tensor_scalar(
                out=t1, in0=ab[:, 2:FW], scalar1=vcol(cw + 2 * NFC + c), scalar2=None, op0=ALU.mult),
                reads=[B_fa[i], B_vecs], writes=[B_ft[i]])
            P.add("dve", lambda h, ab=ab, t1=t1, c=c: h.scalar_tensor_tensor(
                out=t1, in0=ab[:, 1:FW - 1], scalar=vcol(cw + NFC + c), in1=t1, op0=ALU.mult, op1=ALU.add),
                reads=[B_fa[i], B_ft[i], B_vecs], writes=[B_ft[i]])
            P.add("dve", lambda h, ab=ab, t1=t1, c=c: h.scalar_tensor_tensor(
                out=t1, in0=ab[:, 0:T], scalar=vcol(cw + c), in1=t1, op0=ALU.mult, op1=ALU.add),
                reads=[B_fa[i], B_ft[i], B_vecs], writes=[B_ft[i]])
            P.add("act", lambda h, t1=t1, s_=s_, c=c: h.activation(
                out=s_, in_=t1, func=AF.Silu, bias=vcol(V_CB + layer * NFC + c), scale=1.0),
                reads=[B_ft[i], B_vecs], writes=[B_fs[i]])
            P.add("dve", lambda h, s_=s_, b_ps=b_ps, c=c: h.tensor_tensor(
                out=gated[:, c, :], in0=s_, in1=b_ps[:, :], op=ALU.mult),
                reads=[B_fs[i], b_b], writes=[B_gated[c]])

        for c in range(NFC):
            ffn_c(c)
        if mid_hook is not None:
            mid_hook()
        out_proj_ln(f"dn{layer}", NFC, gated, B_gated, V_LN2G + layer * 16, V_LN2B + layer * 16, write_hTb=write_hTb)

    def ybuf(off, n=T):
        return Y[:, off:off + n]

    hg_fg = [ybuf(0), ybuf(512)]
    hg_cum = [ybuf(1024), ybuf(1536)]
    hg_kT = [ybuf(2048), ybuf(2560)]
    hg_e1 = [ybuf(3072), ybuf(3584)]
    hg_e2 = [ybuf(4096), ybuf(4608)]
    hg_qs = [ybuf(5120), ybuf(5632)]
    hg_gs = [ybuf(6144), ybuf(6656)]
    def ybf(off, n=T):
        return Y[:, off:off + n // 2].bitcast(BF16)

    hg_qt = [ybf(7168), ybf(7424)]
    hg_kt = [ybf(7680), ybf(7936)]
    hg_ib = [ybf(8192), ybf(8192)]
    hg_sq_t = sb("hg_sq", (128, T), BF16)
    hg_sq = [hg_sq_t[:, :], hg_sq_t[:, :]]
    hg_tok = [Y[:, 8448:9216].bitcast(BF16), Y[:, 8448:9216].bitcast(BF16)]
    hg_sm = ln_rb[3]
    hg_el = sb("hg_el", (128, 2, 8), F32)
    hg_tmp = ln_rsq[3][:, :].bitcast(F32).rearrange("p (a b) -> p a b", b=128)
    hg_rs = ln_t[0]
    hg_A = ln_t[1]
    names = ["fg", "cum", "kT", "e1", "e2", "qs", "gs", "qt", "kt"]
    B_hg = {n: [Buf(f"hg_{n}0"), Buf(f"hg_{n}1")] for n in names}
    for n in ["ib", "tok"]:
        b_ = Buf(f"hg_{n}")
        B_hg[n] = [b_, b_]
    b_ = Buf("hg_sq")
    B_hg["sq"] = [b_, b_]
    B_sm = Buf("hg_sm")
    B_el = [Buf("hg_el0"), Buf("hg_el1")]
    B_tmp = [Buf("hg_tmp0"), Buf("hg_tmp1")]
    B_rs = Buf("hg_rs")
    B_A = Buf("hg_A")

    import os
    HGS = int(os.environ.get('HGS', '99'))
    NOOUT = int(os.environ.get('NOOUT', '0'))

    def hgrn2(mid_hook=None):
        PB = [(pbank[j], pbank_b[j]) for j in range(4)]
        D0, D1, OB = (pbank[4], pbank_b[4]), (pstat[0], pstat_b[0]), (pstat[1], pstat_b[1])
        B_dsb = [Buf("dsb0"), Buf("dsb1")]
        dsb = [ln_mean, ln_var]
        tk = hg_tok[0]
        P.add("dve", lambda h: h.memset(tk[64:128, 512:1024], 0.0), reads=[B_Y], writes=[B_hg["tok"][0]])
        P.add("dve", lambda h: h.memset(tk[0:64, 1024:1536], 0.0), reads=[B_Y], writes=[B_hg["tok"][0]])

        def proj_slices(hd):
            sA, sAb = load_piece("hg_in", 0, 16, hd * 512)
            sB, sBb = load_piece("hg_in", 0, 16, hd * 512 + 256)
            order = [(PB[1], sA, sAb, 1), (PB[0], sA, sAb, 0), (PB[2], sB, sBb, 0), (PB[3], sB, sBb, 1)]
            sl = []
            for (bank, bank_b), slot, slot_b, m in order:
                for half in range(4):
                    def f_(bank=bank, bank_b=bank_b, slot=slot, slot_b=slot_b, m=m, half=half):
                        for k in range(half * 4, half * 4 + 4):
                            P.add("pe", lambda h, k=k: h.matmul(bank[:, :], lhsT=slot[:, k, m * 128:(m + 1) * 128],
                                                               rhs=hTb[:, k, :], start=(k == 0), stop=(k == 15)),
                                  reads=[slot_b, B_hTb[k]], writes=[bank_b])
                    sl.append(f_)
            return sl

        def gate_stages(hd):
            i = hd % 2
            (q_ps, q_b), (f_ps, f_b), (i_ps, i_b), (g_ps, g_b) = PB
            fg, cum, kT, e1, e2, qs, gs = hg_fg[i], hg_cum[i], hg_kT[i], hg_e1[i], hg_e2[i], hg_qs[i], hg_gs[i]
            qt, kt, ib = hg_qt[i], hg_kt[i], hg_ib[i]
            Bf = {n: B_hg[n][i] for n in B_hg}
            lbc = lbv[:, 0, hd:hd + 1]
            lb1 = lbv[:, 1, hd:hd + 1]
            el = hg_el[:, i, :]

            def ev():
                P.add("act", lambda h: h.activation(out=fg, in_=f_ps[:, :], func=AF.Sigmoid),
                      reads=[f_b, B_Y], writes=[Bf["fg"]])
                P.add("act", lambda h: h.activation(out=qs, in_=q_ps[:, :], func=AF.Silu), reads=[q_b], writes=[Bf["qs"]])
                P.add("act", lambda h: h.activation(out=ib, in_=i_ps[:, :], func=AF.Copy), reads=[i_b], writes=[Bf["ib"]])
                P.add("act", lambda h: h.activation(out=gs, in_=g_ps[:, :], func=AF.Silu), reads=[g_b], writes=[Bf["gs"]])

            def restA():
                P.add("dve", lambda h: h.tensor_scalar(out=fg, in0=fg, scalar1=lb1, scalar2=lbc, op0=ALU.mult, op1=ALU.add),
                      reads=[Bf["fg"], B_lb], writes=[Bf["fg"]])
                P.add("act", lambda h: h.activation(out=cum, in_=fg, func=AF.Ln), reads=[Bf["fg"]], writes=[Bf["cum"]])
                P.add("dve", lambda h: h.tensor_scalar(out=kT, in0=fg, scalar1=-1.0, scalar2=1.0, op0=ALU.mult, op1=ALU.add),
                      reads=[Bf["fg"]], writes=[Bf["kT"]])
                P.add("dve", lambda h: h.tensor_tensor_scan(out=cum, data0=cstb[:, CB_SCAN:CB_SCAN + T], data1=cum,
                                                            initial=0.0, op0=ALU.mult, op1=ALU.add),
                      reads=[Bf["cum"], B_cstb], writes=[Bf["cum"]])

            def restB():
                P.add("act", lambda h: h.activation(out=e1, in_=cum, func=AF.Exp), reads=[Bf["cum"]], writes=[Bf["e1"]])
                P.add("act", lambda h: h.activation(out=e2, in_=cum, func=AF.Exp, scale=-1.0),
                      reads=[Bf["cum"]], writes=[Bf["e2"]])
                P.add("act", lambda h: h.activation(out=el, in_=e1.rearrange("p (c s) -> p c s", s=64)[:, :, 63], func=AF.Copy),
                      reads=[Bf["e1"]], writes=[B_el[i]])

            def restC():
                P.add("dve", lambda h: h.tensor_tensor(out=kt, in0=kT, in1=e2, op=ALU.mult),
                      reads=[Bf["kT"], Bf["e2"]], writes=[Bf["kt"]])
                P.add("dve", lambda h: h.tensor_tensor(out=qt, in0=qs, in1=e1, op=ALU.mult),
                      reads=[Bf["qs"], Bf["e1"]], writes=[Bf["qt"]])
            return ev, restA, restB, restC

        def tphase(hd):
            i = hd % 2
            qt, kt, ib, tok = hg_qt[i], hg_kt[i], hg_ib[i], hg_tok[i]
            Bf = {n: B_hg[n][i] for n in B_hg}
            for tb in range(4):
                P.add("pe", lambda h, tb=tb: h.transpose(out=pbt[:, tb * 128:(tb + 1) * 128],
                                                         in_=ib[:, tb * 128:(tb + 1) * 128], identity=ident_bf[:, :]),
                      reads=[Bf["ib"], B_id], writes=[pbt_b])
            for tb in range(4):
                P.add("pe", lambda h, tb=tb: h.transpose(out=pbt[:, 512 + tb * 128:512 + (tb + 1) * 128],
                                                         in_=kt[:, tb * 128:(tb + 1) * 128], identity=ident_bf[:, :]),
                      reads=[Bf["kt"], B_id], writes=[pbt_b])
            P.add("dve", lambda h: h.tensor_copy(out=tok[:, 0:512], in_=pbt[:, 0:512]), reads=[pbt_b], writes=[Bf["tok"]])
            P.add("dve", lambda h: h.tensor_copy(out=tok[0:64, 512:1024], in_=pbt[0:64, 512:1024]),
                  reads=[pbt_b], writes=[Bf["tok"]])
            P.add("dve", lambda h: h.tensor_copy(out=tok[64:128, 1024:1536], in_=pbt[64:128, 512:1024]),
                  reads=[pbt_b], writes=[Bf["tok"]])
            P.add("act", lambda h: h.activation(out=state_bf[:, 0, :], in_=state[:, hd, :], func=AF.Copy),
                  reads=[B_state[hd]], writes=[B_statebf[0]])
            s_ps, s_b = OB
            for j in range(4):
                P.add("pe", lambda h, j=j: h.matmul(s_ps[:, j * 128:(j + 1) * 128], lhsT=kt[:, j * 128:(j + 1) * 128],
                                                    rhs=qt[:, j * 128:(j + 1) * 128], start=True, stop=True),
                      reads=[Bf["kt"], Bf["qt"]], writes=[s_b])
            P.add("dve", lambda h: h.tensor_tensor(out=hg_sm[:, :], in0=s_ps[:, :], in1=cstb[:, CB_MASKT:CB_MASKT + 512],
                                                   op=ALU.mult), reads=[s_b, B_cstb], writes=[B_sm])
            for c in range(8):
                p, j = c % 2, c // 2
                dp, dpb = (D0, D1)[c // 4]
                P.add("pe", lambda h, c=c, p=p, j=j, dp=dp: h.matmul(
                    dp[:, (c % 4) * 128:(c % 4 + 1) * 128],
                    lhsT=tok[:, 512 + p * 512 + j * 128:512 + p * 512 + (j + 1) * 128],
                    rhs=tok[:, j * 128:(j + 1) * 128], start=True, stop=True), reads=[Bf["tok"]], writes=[dpb])

        def cphase(hd, nslices, nstages):
            i = hd % 2
            qt, tok = hg_qt[i], hg_tok[i]
            Bf = {n: B_hg[n][i] for n in B_hg}
            o_ps, o_b = OB
            for c in range(8):
                p, j = c % 2, c // 2
                dp, dpb = (D0, D1)[c // 4]
                if p == 0:
                    P.add("pe", lambda h, j=j: h.matmul(o_ps[:, j * 128:(j + 1) * 128], lhsT=tok[:, j * 128:(j + 1) * 128],
                                                        rhs=hg_sm[:, j * 128:(j + 1) * 128], start=True, stop=False),
                          reads=[Bf["tok"], B_sm], writes=[o_b])
                P.add("pe", lambda h, c=c: h.matmul(o_ps[:, c * 64:(c + 1) * 64], lhsT=state_bf[:, c % 4, :],
                                                    rhs=qt[:, c * 64:(c + 1) * 64], start=False, stop=(c % 2 == 1)),
                      reads=[B_statebf[c % 4], Bf["qt"]], writes=[o_b])
                if nslices is not None:
                    nslices[c]()
                tm = hg_tmp[:, c % 2, :]
                P.add("dve", lambda h, c=c, dp=dp, tm=tm: h.tensor_tensor(
                    out=tm, in0=dp[:, (c % 4) * 128:(c % 4 + 1) * 128], in1=state[:, hd, :], op=ALU.add),
                    reads=[dpb, B_state[hd]], writes=[B_tmp[c % 2]])
                esc = hg_el[:, i, c:c + 1]
                if c < 7:
                    P.add("act", lambda h, c=c, tm=tm, esc=esc: h.activation(
                        out=state_bf[:, (c + 1) % 4, :], in_=tm, func=AF.Copy, scale=esc),
                        reads=[B_tmp[c % 2], B_el[i]], writes=[B_statebf[(c + 1) % 4]])
                P.add("dve", lambda h, tm=tm, esc=esc: h.tensor_scalar(
                    out=state[:, hd, :], in0=tm, scalar1=esc, scalar2=None, op0=ALU.mult),
                    reads=[B_tmp[c % 2], B_el[i]], writes=[B_state[hd]])

        def r1(hd):
            i = hd % 2
            sq = hg_sq[i]
            Bf = {n: B_hg[n][i] for n in B_hg}
            o_ps, o_b = OB
            ss_ps, ss_b = D0
            P.add("act", lambda h: h.activation(out=sq, in_=o_ps[:, :], func=AF.Square), reads=[o_b], writes=[Bf["sq"]])
            P.add("pe", lambda h: h.matmul(ss_ps[:, :], lhsT=ones_bf[:, :], rhs=sq, start=True, stop=True),
                  reads=[Bf["sq"], B_id], writes=[ss_b])

        def r2a(hd):
            ss_ps, ss_b = D0
            P.add("act", lambda h: h.activation(out=hg_rs[:, :], in_=ss_ps[:, :], func=AF.Ln,
                                                bias=eps_col[:, 0:1], scale=1.0), reads=[ss_b, B_id], writes=[B_rs])
            P.add("act", lambda h: h.activation(out=hg_rs[:, :], in_=hg_rs[:, :], func=AF.Exp, scale=-0.5),
                  reads=[B_rs], writes=[B_rs])

        def r2b(hd):
            i = hd % 2
            gs = hg_gs[i]
            Bf = {n: B_hg[n][i] for n in B_hg}
            o_ps, o_b = OB
            P.add("dve", lambda h: h.tensor_tensor(out=hg_A[:, :], in0=o_ps[:, :], in1=hg_rs[:, :], op=ALU.mult),
                  reads=[o_b, B_rs], writes=[B_A])
            ngc = lbv[:, 2, hd:hd + 1]
            P.add("dve", lambda h: h.scalar_tensor_tensor(out=yT[:, hd, :], in0=hg_A[:, :], scalar=ngc, in1=gs,
                                                          op0=ALU.mult, op1=ALU.mult),
                  reads=[B_A, Bf["gs"], B_lb], writes=[B_y[hd]])

        for f_ in proj_slices(0):
            f_()
        gates = {hd: gate_stages(hd) for hd in range(16)}
        for f_ in gates[0]:
            f_()
        for f_ in proj_slices(1):
            f_()
        for hd in range(16):
            tphase(hd)
            nx = gates[hd + 1] if hd + 1 < 16 else None
            if nx is not None:
                nx[0]()
            nsl = proj_slices(hd + 2) if hd + 2 < 16 else None
            cphase(hd, nsl[0:8] if nsl is not None else None, None)
            r1(hd)
            if nsl is not None:
                for f_ in nsl[8:]:
                    f_()
            if nx is not None:
                nx[1]()
            r2a(hd)
            if nx is not None:
                nx[2]()
            r2b(hd)
            if nx is not None:
                nx[3]()
            if hd == 3 and mid_hook is not None:
                mid_hook()
        out_proj_ln("hg_out", 16, yT, B_y, V_LN1G, V_LN1B)

    vT = Y[:, 0:8192].rearrange("p (c t) -> p c t", t=T)
    B_v = [Buf(f"v{c}") for c in range(16)]
    g_vn = [Y[:, 8192:8448].bitcast(BF16), Y[:, 8448:8704].bitcast(BF16)]
    g_tok = [Y[:, 8704:8960].bitcast(BF16), Y[:, 8960:9216].bitcast(BF16)]
    B_gvn = [Buf("gvn0"), Buf("gvn1")]
    B_gtok = [Buf("gtok0"), Buf("gtok1")]
    g_t2 = [ln_t[0], ln_t[1]]

    def gmlp():
        st = ln_stats_begin()
        pre = {}
        grp = []
        for j in (0, 1):
            slot, slot_b = load_piece("sg_in", 0, 16, j * WC)
            b0 = get_bank()
            b1 = get_bank()
            pre[j] = (b0, b1)
            grp += [(b0[0], b0[1], slot, slot_b, 0), (b1[0], b1[1], slot, slot_b, 1)]
        mm_kouter(grp, hTb, B_hTb)
        pend = []
        for j in range(8):
            if j in pre:
                banks = pre[j]
            else:
                slot, slot_b = load_piece("sg_in", 0, 16, j * WC)
                banks = (get_bank(), get_bank())
                for m in range(2):
                    mm_group(banks[m][0], banks[m][1], slot, slot_b, m, 0, 16, 16, hTb, B_hTb)
            newp = []
            for m in range(2):
                c = 2 * j + m
                v_ps, v_b = banks[m]
                P.add("act", lambda h, c=c, v_ps=v_ps: h.activation(out=vT[:, c, :], in_=v_ps[:, :], func=AF.Gelu),
                      reads=[v_b, B_Y], writes=[B_v[c]])
                newp.append(ln_stats_add(st, vT[:, c, :], B_v[c], c, 16))
            for f_ in pend:
                f_()
            pend = newp
        for f_ in pend:
            f_()
        ln_stats_finish(st, D, LN_EPS)

        def lnap(j):
            for m in range(2):
                c = 2 * j + m
                ln_apply(vT[:, c, :], B_v[c], vcol(V_SGG + c), vcol(V_SGB + c), [(g_vn[m], B_gvn[m])])

        tok2 = Y[:, 8704:9216].bitcast(BF16)
        lnap(0)

        def pair(j):
            for m in range(2):
                vn = g_vn[m]
                for tb in range(4):
                    P.add("pe", lambda h, tb=tb, vn=vn, m=m: h.transpose(
                        out=pbt[:, m * 512 + tb * 128:m * 512 + (tb + 1) * 128],
                        in_=vn[:, tb * 128:(tb + 1) * 128], identity=ident_bf[:, :]),
                        reads=[B_gvn[m], B_id], writes=[pbt_b])
            P.add("dve", lambda h: h.tensor_copy(out=tok2, in_=pbt[:, :]), reads=[pbt_b], writes=[B_gtok[0], B_gtok[1]])
            if j + 1 < 8:
                lnap(j + 1)
            slot, slot_b = load_piece("sg_in", 0, 16, 2048 + j * WC)
            ub = (get_bank(), get_bank())
            for m in range(2):
                mm_group(ub[m][0], ub[m][1], slot, slot_b, m, 0, 16, 16, hTb, B_hTb)
            for m in range(2):
                c = 2 * j + m
                u_ps, u_b = ub[m]
                P.add("act", lambda h, c=c, u_ps=u_ps: h.activation(out=uT[:, c, :], in_=u_ps[:, :], func=AF.Gelu),
                      reads=[u_b], writes=[B_u[c]])
            gb = (get_bank(), get_bank())
            for m in range(2):
                c = 2 * j + m
                g_ps, g_b = gb[m]
                for tb in range(4):
                    P.add("pe", lambda h, tb=tb, m=m, c=c, g_ps=g_ps: h.matmul(
                        g_ps[:, tb * 128:(tb + 1) * 128], lhsT=tok2[:, m * 512 + tb * 128:m * 512 + (tb + 1) * 128],
                        rhs=wcT[:, c, :], start=True, stop=False), reads=[B_gtok[m], B_wc], writes=[g_b])
                    P.add("pe", lambda h, tb=tb, c=c, g_ps=g_ps: h.matmul(
                        g_ps[:, tb * 128:(tb + 1) * 128], lhsT=ones_f[:, :], rhs=bs_row[:, c * 128:(c + 1) * 128],
                        start=False, stop=True), reads=[B_misc, B_id], writes=[g_b])
            for m in range(2):
                c = 2 * j + m
                g_ps, g_b = gb[m]
                P.add("dve", lambda h, c=c, g_ps=g_ps: h.tensor_tensor(out=yT[:, c, :], in0=g_ps[:, :], in1=uT[:, c, :],
                                                                      op=ALU.mult), reads=[g_b, B_u[c]], writes=[B_y[c]])

        for j in range(8):
            pair(j)
        out_proj_ln("sg_out", 16, yT, B_y, V_LN1G + 16, V_LN1B + 16)

    def cast_h():
        for dc in range(NDC):
            P.add("act", lambda h, dc=dc: h.activation(out=hTb[:, dc, :], in_=hT[:, dc, :], func=AF.Copy),
                  reads=[B_hT[dc]], writes=[B_hTb[dc]])

    nlayers = 4 if dbg is None else dbg[0]
    assert nlayers == 4
    for ti in range(nt):
        cur_tile[0] = ti
        hgrn2(mid_hook=(lambda ti=ti: load_x(ti)) if ti > 0 else None)
        ffn(0, ti)
        gmlp()
        hook = (lambda ti=ti: load_xb(ti + 1)) if ti + 1 < nt else None
        ffn(1, ti, mid_hook=hook, write_hTb=False)
        for g in range(4):
            dst = outT.rearrange("(c p) t -> p c t", p=128)[:, 4 * g:4 * g + 4, ti * T:(ti + 1) * T]
            P.add("act", lambda h, g=g, dst=dst: h.dma_start(out=dst, in_=hT[:, 4 * g:4 * g + 4, :]),
                  reads=B_hT[4 * g:4 * g + 4], dma=f"o{g}", is_out=True)

    import contextlib
    with contextlib.ExitStack() as es:
        sems = {e: es.enter_context(nc.semaphore("sem_" + e)) for e in Prog.ENGS}
        dsems = {k: es.enter_context(nc.semaphore("dsem_" + str(n))) for n, k in enumerate(P.dma_counts)}
        P.emit(nc, sems, dsems)
    return nc


def _cols(v, nch):
    return np.ascontiguousarray(np.asarray(v, np.float32).reshape(nch, 128).T)


def make_consts():
    c = np.zeros((128, NCST), np.float32)
    c[:, C_ID:C_ID + 128] = np.eye(128, dtype=np.float32)
    c[:, C_ONES:C_ONES + 128] = 1.0
    s = np.arange(128)[:, None]
    tt = np.arange(128)[None, :]
    c[:, C_TRI:C_TRI + 128] = (s <= tt).astype(np.float32)
    return c


def make_constsb():
    c = np.zeros((128, NCSTB), np.float32)
    p = np.arange(128)[:, None]
    t = np.arange(128)[None, :]
    m = ((p <= t) & ((p // 64) == (t // 64))).astype(np.float32)
    c[:, CB_MASKT:CB_MASKT + 512] = np.tile(m, (1, 4))
    sc = np.ones((128, 512), np.float32)
    sc[:, ::64] = 0.0
    c[:, CB_SCAN:CB_SCAN + 512] = sc
    return c


def prep_shared(inp):
    f = lambda a: np.asarray(a, np.float32)
    d = {}
    d["consts"] = make_consts()
    d["constsb"] = make_constsb()
    parts = []
    lbl = f(inp["lb_logits"])
    parts += [_cols(lbl[r], 16) for r in range(3)]
    parts.append(_cols(f(inp["hg_norm_g"])[0], 16))
    parts.append(_cols(f(inp["sg_ln_g"])[0], 16))
    parts.append(_cols(f(inp["sg_ln_b"])[0], 16))
    for k in ("ln1_g", "ln1_b", "ln2_g", "ln2_b"):
        a = f(inp[k])
        parts += [_cols(a[0], 16), _cols(a[1], 16)]
    cw = f(inp["ffn_conv_w"])
    for l in range(2):
        for j in range(3):
            parts.append(_cols(cw[l, j], NFC))
    cb = f(inp["ffn_conv_b"])
    parts += [_cols(cb[0], NFC), _cols(cb[1], NFC)]
    vec = np.concatenate(parts, axis=1)
    assert vec.shape == (128, NV), vec.shape
    d["vecs"] = np.ascontiguousarray(vec)
    ws = f(inp["sg_w_s"])[0]
    d["sg_w_sT"] = np.ascontiguousarray(ws.transpose(2, 0, 1).reshape(128, 16 * 128))
    d["sg_b_row"] = np.ascontiguousarray(f(inp["sg_b_s"])[0].reshape(1, 2048))
    w = f(inp["hg_w_in"])[0]
    d["w_hg_in"] = np.ascontiguousarray(w.reshape(D, 4, 16, 128).transpose(0, 2, 1, 3).reshape(D, 8192))
    d["w_hg_out"] = np.ascontiguousarray(f(inp["hg_w_out"])[0])
    w = f(inp["sg_w_in"])[0]
    d["w_sg_in"] = np.ascontiguousarray(np.concatenate([w[:, 2048:], w[:, :2048]], axis=1))
    d["w_sg_out"] = np.ascontiguousarray(f(inp["sg_w_out"])[0])
    for l in range(2):
        w = f(inp["ffn_w_up"])[l]
        d[f"w_up{l}"] = np.ascontiguousarray(w.reshape(D, 2, NFC, 128).transpose(0, 2, 1, 3).reshape(D, 2 * DFF))
        d[f"w_dn{l}"] = np.ascontiguousarray(f(inp["ffn_w_down"])[l])
    return d


_NC_CACHE = {}


def kernel(**inputs):
    x = np.asarray(inputs["x"], np.float32)
    B = x.shape[0]
    shared = prep_shared(inputs)
    if "nc" not in _NC_CACHE:
        _NC_CACHE["nc"] = build(S_FULL // T)
    nc = _NC_CACHE["nc"]
    in_maps = []
    for b in range(B):
        m = dict(shared)
        m["xT"] = np.ascontiguousarray(x[b].T)
        in_maps.append(m)
    res = run_bass_kernel_spmd(nc, in_maps, core_ids=list(range(B)))
    out = np.stack([np.ascontiguousarray(res.results[b]["outT"].T) for b in range(B)], axis=0)
    return out.astype(np.float32)
```
